# Optimizing a Trainium2 kernel written in Bass

```python
import math
import jax, jax.numpy as jnp
from jax import lax
import numpy as np

D_MODEL = 1024
BATCH = 8
SEQ = 2048
DEPTH = 2

HEAD_DIM = 64
ROPE_THETA = 10000.0
A_Q_HEADS = 8
A_KV_HEADS = 2
A_WINDOW = 128
A_BLOCK = 128
B_Q_HEADS = 8
B_Q_RANK = 256
B_IDX_HEADS = 8
B_IDX_DIM = HEAD_DIM
B_TOPK_MAX = 256
B_QBLOCK = 128
C_HEADS = 16
C_BLOCK = 256
C_TOPK = 3
C_QCHUNK = 16
D_FF = 4 * D_MODEL
N_EVEN = (DEPTH + 1) // 2
N_ODD = DEPTH // 2
DN_ALPHA = (2 * DEPTH) ** 0.25
DN_BETA = (8 * DEPTH) ** -0.25
LN_EPS = 1e-5
RMS_EPS = 1e-6

AB_WIDTHS = (A_Q_HEADS * HEAD_DIM, A_KV_HEADS * HEAD_DIM, A_KV_HEADS * HEAD_DIM,
             B_Q_RANK, HEAD_DIM, HEAD_DIM, B_IDX_DIM, B_IDX_HEADS)
AB_IN_WIDTH = sum(AB_WIDTHS)
AB_MIX_WIDTH = (A_Q_HEADS + B_Q_HEADS) * HEAD_DIM
C_WIDTH = C_HEADS * HEAD_DIM

kernel_name = "hybrid_swa_dsa_moba_deepnorm_adaln"


def _split_widths(t, widths):
    return jnp.split(t, np.cumsum(widths)[:-1].tolist(), axis=-1)


def layer_norm(x, g, b):
    xf = x.astype(jnp.float32)
    mu = jnp.mean(xf, axis=-1, keepdims=True)
    var = jnp.mean(jnp.square(xf - mu), axis=-1, keepdims=True)
    return ((xf - mu) * lax.rsqrt(var + LN_EPS) * g.astype(jnp.float32) + b.astype(jnp.float32)).astype(x.dtype)


def rms_norm(x, g):
    xf = x.astype(jnp.float32)
    ms = jnp.mean(jnp.square(xf), axis=-1, keepdims=True)
    return (xf * lax.rsqrt(ms + RMS_EPS) * g.astype(jnp.float32)).astype(x.dtype)


def rope_tables(positions):
    inv = ROPE_THETA ** (-jnp.arange(0, HEAD_DIM, 2, dtype=jnp.float32) / HEAD_DIM)
    ang = positions.astype(jnp.float32)[..., None] * inv
    ang = jnp.concatenate([ang, ang], axis=-1)
    return jnp.cos(ang)[:, :, None, :], jnp.sin(ang)[:, :, None, :]


def apply_rope(t, cos, sin):
    t1, t2 = jnp.split(t, 2, axis=-1)
    rot = jnp.concatenate([-t2, t1], axis=-1)
    return (t * cos + rot * sin).astype(t.dtype)


def ada_modulation(c, w, b):
    m = jax.nn.silu(c) @ w + b
    shift, scale, gate = jnp.split(m[:, None, :], 3, axis=-1)
    return shift, scale, 1.0 + gate


def swa_sink_attention(q, k, v, sinks):
    Bn, S, HQ, hd = q.shape
    HKV = k.shape[2]
    G = HQ // HKV
    nb = S // A_BLOCK
    qb = q.reshape(Bn, nb, A_BLOCK, HKV, G, hd)

    def band(t):
        tb = t.reshape(Bn, nb, A_BLOCK, HKV, hd)
        prev = jnp.pad(tb[:, :-1], ((0, 0), (1, 0), (0, 0), (0, 0), (0, 0)))
        return jnp.concatenate([prev, tb], axis=2)

    kb, vb = band(k), band(v)
    s = jnp.einsum('bnqkgd,bnskd->bnkgqs', qb, kb, preferred_element_type=jnp.float32) * (hd ** -0.5)
    qpos = jnp.arange(nb)[:, None, None] * A_BLOCK + jnp.arange(A_BLOCK)[None, :, None]
    kpos = jnp.arange(nb)[:, None, None] * A_BLOCK - A_BLOCK + jnp.arange(2 * A_BLOCK)[None, None, :]
    diff = qpos - kpos
    mask = (diff >= 0) & (diff < A_WINDOW) & (kpos >= 0)
    s = jnp.where(mask[None, :, None, None], s, -jnp.inf)
    sink = jnp.broadcast_to(sinks.astype(jnp.float32).reshape(HKV, G)[None, None, :, :, None, None],
                            s.shape[:-1] + (1,))
    p = jax.nn.softmax(jnp.concatenate([s, sink], axis=-1), axis=-1)[..., :-1]
    o = jnp.einsum('bnkgqs,bnskd->bnqkgd', p.astype(vb.dtype), vb)
    return o.reshape(Bn, S, HQ * hd)


def dsa_attention(q, k, v, iq, ik, iw):
    Bn, S, H, hd = q.shape
    n_top = min(B_TOPK_MAX, S // 4)
    nqb = S // B_QBLOCK

    def to_blocks(t):
        return jnp.moveaxis(t.reshape((Bn, nqb, B_QBLOCK) + t.shape[2:]), 1, 0)

    kpos = jnp.arange(S)
    bidx = jnp.arange(Bn)[:, None, None]
    scale = hd ** -0.5

    def one_block(args):
        i, qb, iqb, iwb = args
        tpos = i * B_QBLOCK + jnp.arange(B_QBLOCK)
        raw = jnp.einsum('bqhd,bsd->bhqs', iqb, ik, preferred_element_type=jnp.float32)
        score = jnp.einsum('bhqs,bqh->bqs', jax.nn.relu(raw), iwb.astype(jnp.float32))
        causal = kpos[None, :] <= tpos[:, None]
        score = jnp.where(causal[None], score, -jnp.inf)
        _, idx = lax.top_k(score, n_top)
        valid = idx <= tpos[None, :, None]
        kg = k[bidx, idx]
        vg = v[bidx, idx]
        s = jnp.einsum('bqhd,bqkd->bhqk', qb, kg, preferred_element_type=jnp.float32) * scale
        s = jnp.where(valid[:, None], s, -jnp.inf)
        p = jax.nn.softmax(s, axis=-1)
        o = jnp.einsum('bhqk,bqkd->bqhd', p.astype(vg.dtype), vg)
        return o.reshape(Bn, B_QBLOCK, H * hd)

    out = lax.map(one_block, (jnp.arange(nqb), to_blocks(q), to_blocks(iq), to_blocks(iw)))
    return jnp.moveaxis(out, 0, 1).reshape(Bn, S, H * hd)


def moba_attention(q, k, v):
    Bn, S, H, hd = q.shape
    nb = -(-S // C_BLOCK)
    Sp = nb * C_BLOCK
    pad = ((0, 0), (0, Sp - S), (0, 0), (0, 0))
    kb = jnp.pad(k, pad).reshape(Bn, nb, C_BLOCK, H, hd).transpose(0, 3, 1, 2, 4)
    vb = jnp.pad(v, pad).reshape(Bn, nb, C_BLOCK, H, hd).transpose(0, 3, 1, 2, 4)
    kmean = jnp.mean(kb.astype(jnp.float32), axis=3)
    n_sel = min(C_TOPK, nb - 1)
    nqc = S // C_QCHUNK
    qc = q.reshape(Bn, nqc, C_QCHUNK, H, hd).transpose(1, 0, 3, 2, 4)
    bi = jnp.arange(Bn)[:, None, None, None]
    hi = jnp.arange(H)[None, :, None, None]
    scale = hd ** -0.5

    def one_chunk(args):
        i, qblk = args
        tpos = i * C_QCHUNK + jnp.arange(C_QCHUNK)
        own = tpos // C_BLOCK
        own_idx = jnp.broadcast_to(own[None, None, :, None], (Bn, H, C_QCHUNK, 1))
        own_ok = jnp.ones((Bn, H, C_QCHUNK, 1), dtype=bool)
        if n_sel > 0:
            gate = jnp.einsum('bhqd,bhnd->bhqn', qblk, kmean, preferred_element_type=jnp.float32)
            past = jnp.arange(nb)[None, :] < own[:, None]
            gate = jnp.where(past[None, None], gate, -jnp.inf)
            _, sel = lax.top_k(gate, n_sel)
            sel_ok = sel < own[None, None, :, None]
            blk = jnp.concatenate([sel, own_idx], axis=-1)
            blk_ok = jnp.concatenate([sel_ok, own_ok], axis=-1)
        else:
            blk, blk_ok = own_idx, own_ok
        kg = kb[bi, hi, blk]
        vg = vb[bi, hi, blk]
        s = jnp.einsum('bhqd,bhqnjd->bhqnj', qblk, kg, preferred_element_type=jnp.float32) * scale
        kpos = blk[..., None] * C_BLOCK + jnp.arange(C_BLOCK)
        ok = blk_ok[..., None] & (kpos <= tpos[None, None, :, None, None])
        s = jnp.where(ok, s, -jnp.inf)
        p = jax.nn.softmax(s.reshape(s.shape[:3] + (-1,)), axis=-1).reshape(s.shape)
        o = jnp.einsum('bhqnj,bhqnjd->bqhd', p.astype(vg.dtype), vg)
        return o.reshape(Bn, C_QCHUNK, H * hd)

    out = lax.map(one_chunk, (jnp.arange(nqc), qc))
    return jnp.moveaxis(out, 0, 1).reshape(Bn, S, H * hd)


def mixer_ab(h, cos, sin, w_in, q_norm, w_uq, w_uiq, sinks, w_out):
    Bn, S, _ = h.shape
    aq, ak, av, cq, bk, bv, bik, biw = _split_widths(h @ w_in, AB_WIDTHS)
    aq = apply_rope(aq.reshape(Bn, S, A_Q_HEADS, HEAD_DIM), cos, sin)
    ak = apply_rope(ak.reshape(Bn, S, A_KV_HEADS, HEAD_DIM), cos, sin)
    av = av.reshape(Bn, S, A_KV_HEADS, HEAD_DIM)
    ya = swa_sink_attention(aq, ak, av, sinks)
    cq = rms_norm(cq, q_norm)
    bq = apply_rope((cq @ w_uq).reshape(Bn, S, B_Q_HEADS, HEAD_DIM), cos, sin)
    biq = apply_rope((cq @ w_uiq).reshape(Bn, S, B_IDX_HEADS, B_IDX_DIM), cos, sin)
    bk = apply_rope(bk[:, :, None, :], cos, sin)[:, :, 0]
    bik = apply_rope(bik[:, :, None, :], cos, sin)[:, :, 0]
    iw = biw * (B_IDX_HEADS ** -0.5 * B_IDX_DIM ** -0.5)
    yb = dsa_attention(bq, bk, bv, biq, bik, iw)
    return jnp.concatenate([ya, yb], axis=-1) @ w_out


def mixer_c(h, cos, sin, w_in, w_out):
    Bn, S, _ = h.shape
    q, k, v = jnp.split(h @ w_in, 3, axis=-1)
    q = apply_rope(q.reshape(Bn, S, C_HEADS, HEAD_DIM), cos, sin)
    k = apply_rope(k.reshape(Bn, S, C_HEADS, HEAD_DIM), cos, sin)
    v = v.reshape(Bn, S, C_HEADS, HEAD_DIM)
    return moba_attention(q, k, v) @ w_out


def squared_relu_mlp(h, w1, w2):
    return jnp.square(jax.nn.relu(h @ w1)) @ w2


def setup_inputs(seed: int = 0) -> dict:
    key = jax.random.key(seed)
    ks = jax.random.split(key, 20)
    f32 = jnp.float32
    n = lambda k, shp, s: jax.random.normal(k, shp, f32) * s
    D = D_MODEL
    return {
        "x": n(ks[0], (BATCH, SEQ, D), 1.0),
        "c": n(ks[1], (BATCH, D), 1.0),
        "positions": jnp.broadcast_to(jnp.arange(SEQ, dtype=jnp.int32)[None, :], (BATCH, SEQ)),
        "ab_w_in": n(ks[2], (N_EVEN, D, AB_IN_WIDTH), D ** -0.5),
        "ab_q_norm": 1.0 + n(ks[3], (N_EVEN, B_Q_RANK), 0.02),
        "ab_w_uq": n(ks[4], (N_EVEN, B_Q_RANK, B_Q_HEADS * HEAD_DIM), B_Q_RANK ** -0.5),
        "ab_w_uiq": n(ks[5], (N_EVEN, B_Q_RANK, B_IDX_HEADS * B_IDX_DIM), B_Q_RANK ** -0.5),
        "ab_sinks": n(ks[6], (N_EVEN, A_Q_HEADS), 1.0),
        "ab_w_out": n(ks[7], (N_EVEN, AB_MIX_WIDTH, D), AB_MIX_WIDTH ** -0.5 * DN_BETA),
        "c_w_in": n(ks[8], (N_ODD, D, 3 * C_WIDTH), D ** -0.5),
        "c_w_out": n(ks[9], (N_ODD, C_WIDTH, D), C_WIDTH ** -0.5 * DN_BETA),
        "ada_w": n(ks[10], (DEPTH, 2, D, 3 * D), 0.1 * D ** -0.5),
        "ada_b": n(ks[11], (DEPTH, 2, 3 * D), 0.02),
        "ln_g": 1.0 + n(ks[12], (DEPTH, 2, D), 0.02),
        "ln_b": n(ks[13], (DEPTH, 2, D), 0.02),
        "mlp_w1": n(ks[14], (DEPTH, D, D_FF), D ** -0.5),
        "mlp_w2": n(ks[15], (DEPTH, D_FF, D), D_FF ** -0.5 * DN_BETA),
    }


def reference(x, c, positions, ab_w_in, ab_q_norm, ab_w_uq, ab_w_uiq, ab_sinks, ab_w_out,
              c_w_in, c_w_out, ada_w, ada_b, ln_g, ln_b, mlp_w1, mlp_w2):
    cos, sin = rope_tables(positions)
    for layer in range(DEPTH):
        shift, scale, gate = ada_modulation(c, ada_w[layer, 0], ada_b[layer, 0])
        h = x * (1.0 + scale) + shift
        if layer % 2 == 0:
            e = layer // 2
            y = mixer_ab(h, cos, sin, ab_w_in[e], ab_q_norm[e], ab_w_uq[e], ab_w_uiq[e],
                         ab_sinks[e], ab_w_out[e])
        else:
            o = layer // 2
            y = mixer_c(h, cos, sin, c_w_in[o], c_w_out[o])
        x = layer_norm(DN_ALPHA * x + gate * y, ln_g[layer, 0], ln_b[layer, 0])
        shift, scale, gate = ada_modulation(c, ada_w[layer, 1], ada_b[layer, 1])
        h = x * (1.0 + scale) + shift
        y = squared_relu_mlp(h, mlp_w1[layer], mlp_w2[layer])
        x = layer_norm(DN_ALPHA * x + gate * y, ln_g[layer, 1], ln_b[layer, 1])
    return x
```

```python
import numpy as np
from contextlib import ExitStack
import concourse.bass as bass
import concourse.mybir as mybir
from concourse.bass_utils import run_bass_kernel_spmd

F32 = mybir.dt.float32
BF16 = mybir.dt.bfloat16
I32 = mybir.dt.int32
AF = mybir.ActivationFunctionType
ALU = mybir.AluOpType
AX = mybir.AxisListType

S = 2048
D = 1024
NT = 16
NG = 4
NEG = -32768.0
ALPHA = 4.0 ** 0.25
LN_EPS_EFF = 1e-5 / (ALPHA * ALPHA)
ENGS = ["pe", "dve", "act", "pool", "sp"]

C_ID, C_RM, C_TRIA, C_TRIB, C_ALLNEG, C_TRIF, C_ONES = 0, 128, 256, 384, 512, 640, 768
C_POW2, C_INV, C_M1, C_EJ, C_END = 896, 920, 921, 985, 985 + 1024


def _consts():
    c = np.zeros((128, C_END), np.float32)
    c[:, C_ID:C_ID + 128] = np.eye(128)
    for m in range(128):
        if (m % 64) < 32:
            c[m + 32, C_RM + m] = -1.0
        else:
            c[m - 32, C_RM + m] = 1.0
    j = np.arange(128)[:, None]
    i = np.arange(128)[None, :]
    c[:, C_TRIA:C_TRIA + 128] = np.where(i > j, NEG, 0.0)
    c[:, C_TRIB:C_TRIB + 128] = np.where(i <= j, NEG, 0.0)
    c[:, C_ALLNEG:C_ALLNEG + 128] = NEG
    c[:, C_TRIF:C_TRIF + 128] = np.where(i > j, -1e30, 0.0)
    c[:, C_ONES:C_ONES + 128] = 1.0
    c[:, C_POW2:C_POW2 + 24] = 2.0 ** (-(np.arange(24) + 1.0))
    inv = (10000.0 ** (-np.arange(0, 64, 2, dtype=np.float32) / 64)).astype(np.float32)
    c[:, C_INV] = inv[np.arange(128) % 32]
    for own in range(8):
        c[:, C_M1 + own * 8:C_M1 + own * 8 + 8] = np.where(np.arange(8) >= own, -1e30, 0.0)[None, :]
    for j in range(8):
        c[j, C_EJ + j * 128:C_EJ + (j + 1) * 128] = 1.0
    return c


class Prog:
    def __init__(self, nc, stack):
        self.nc = nc
        self.stack = stack
        self.ops = {e: [] for e in ENGS}
        self.sem = {e: stack.enter_context(nc.semaphore("s_" + e)) for e in ENGS}
        self.cnt = {e: 0 for e in ENGS}
        self.known = {e: {} for e in ENGS}
        self.last_w = {}
        self.readers = {}
        self.slots = {}
        self.ninst = 0

    def slot_sem(self, name):
        if name not in self.slots:
            self.slots[name] = [self.stack.enter_context(self.nc.semaphore("d_" + name)), 0]
        return self.slots[name]

    def _wait(self, eng, s, v):
        if self.known[eng].get(s.name, 0) >= v:
            return
        self.known[eng][s.name] = v
        self.ops[eng].append(lambda e: e.wait_ge(s, v))

    def _emit(self, eng, fn, reads, writes, ev_sem):
        writes = list(writes) + [k for k in reads if k.startswith("ps") and k not in writes]
        reads = [k for k in reads if not k.startswith("ps")]
        deps = {}
        for k in list(reads) + list(writes):
            ev = self.last_w.get(k)
            if ev is not None and (ev[0].name not in deps or deps[ev[0].name][1] < ev[1]):
                deps[ev[0].name] = ev
        for k in writes:
            for ev in self.readers.get(k, {}).values():
                if ev[0].name not in deps or deps[ev[0].name][1] < ev[1]:
                    deps[ev[0].name] = ev
        for s, v in deps.values():
            if eng == "pe" and s is self.sem["pe"]:
                continue
            self._wait(eng, s, v)
        if ev_sem is None:
            self.cnt[eng] += 1
            sem, inc = self.sem[eng], 1
            ev = (sem, self.cnt[eng])
        else:
            ev_sem[1] += 16
            sem, inc = ev_sem[0], 16
            ev = (sem, ev_sem[1])
        self.ops[eng].append(lambda e: fn(e).then_inc(sem, inc))
        self.ninst += 1
        for k in writes:
            self.last_w[k] = ev
            self.readers[k] = {}
        for k in reads:
            if k not in writes:
                self.readers.setdefault(k, {})[sem.name] = ev
        return ev

    def pe_sync(self):
        if self.cnt["pe"] > 0:
            self._wait("pe", self.sem["pe"], self.cnt["pe"])

    def op(self, eng, fn, reads=(), writes=()):
        return self._emit(eng, fn, reads, writes, None)

    def dma(self, eng, out, in_, reads=(), writes=(), slot=None):
        if slot is None:
            slot = str(writes[0])
        return self._emit(eng, lambda e: e.dma_start(out=out, in_=in_), reads, writes, self.slot_sem(slot))

    def barrier(self):
        for s, v in self.slots.values():
            self._wait("sp", s, v)
        for e2 in ENGS:
            if e2 != "sp" and self.cnt[e2] > 0:
                self._wait("sp", self.sem[e2], self.cnt[e2])
        self.cnt["sp"] += 1
        sem, v = self.sem["sp"], self.cnt["sp"]
        self.ops["sp"].append(lambda e: e.nop().then_inc(sem, 1))
        for e2 in ENGS:
            if e2 != "sp":
                self._wait(e2, sem, v)

    def flush(self):
        nc = self.nc
        ops = self.ops
        self.ops = {e: [] for e in ENGS}
        with nc.Block() as block:
            @block.tensor
            def _(e):
                for o in ops["pe"]:
                    o(e)

            @block.vector
            def _(e):
                for o in ops["dve"]:
                    o(e)

            @block.scalar
            def _(e):
                for o in ops["act"]:
                    o(e)

            @block.gpsimd
            def _(e):
                for o in ops["pool"]:
                    o(e)

            @block.sync
            def _(e):
                for o in ops["sp"]:
                    o(e)


def build(stop=None, dbg=False):
    nc = bass.Bass("TRN2", target_bir_lowering=False)
    di = lambda n, s, d=F32: nc.dram_tensor(n, s, d, kind="ExternalInput").ap()
    xT_d = di("xT", [D, S])
    pos_d = di("pos", [1, S], I32)
    cvec_d = di("cvec", [128, 8])
    cst_d = di("cst", [128, C_END])
    adaw_d = di("ada_w", [4, D, 3 * D])
    adab_d = di("ada_b", [128, 96])
    lng_d = di("ln_g", [128, 32])
    lnb_d = di("ln_b", [128, 32])
    qn_d = di("qnorm", [128, 2])
    sinks_d = di("sinks", [1, 8])
    win_d = di("ab_w_in", [D, 1224])
    wuq_d = di("ab_w_uq", [256, 512])
    wuiq_d = di("ab_w_uiq", [256, 512])
    wout_d = [di("ab_w_out", [D, D]), di("c_w_out", [D, D])]
    cwin_d = di("c_w_in", [D, 3 * D])
    w1_d = di("mlp_w1", [2, D, 4 * D])
    w2_d = di("mlp_w2", [2, 4 * D, D])
    out_d = nc.dram_tensor("outT", [D, S], F32, kind="ExternalOutput").ap()
    xsp_d = nc.dram_tensor("xspill", [D, S], F32, kind="Internal").ap()
    w1s_d = nc.dram_tensor("w1s", [2, 8, 128, 4096], BF16, kind="Internal").ap()
    w2s_d = nc.dram_tensor("w2s", [2, 8, 128, 4096], BF16, kind="Internal").ap()
    cws_d = nc.dram_tensor("cws", [8, 128, 8 * 384], BF16, kind="Internal").ap()
    wos_d = nc.dram_tensor("wos", [2, 128, 8 * D], BF16, kind="Internal").ap()
    dbg_outs = {}

    uid = [0]

    def alloc(stack, n, shape, d=F32):
        uid[0] += 1
        return stack.enter_context(nc.sbuf_tensor(f"{n}_{uid[0]}", shape, d))

    with ExitStack() as glob:
        P = Prog(nc, glob)
        GT = lambda n, s, d=F32: glob.enter_context(nc.sbuf_tensor(n, s, d))
        cosT = GT("cosT", [128, S])
        sinT = GT("sinT", [128, S])
        cstf = GT("cstf", [128, C_EJ])
        cstb = GT("cstb", [128, 640], BF16)
        id4 = GT("id4", [128, 512], BF16)
        modv = GT("modv", [128, 96])
        mod1 = GT("mod1", [128, 96])
        lng = GT("lng", [128, 32])
        lnb = GT("lnb", [128, 32])
        qn = GT("qn", [128, 2])
        esink = GT("esink", [128, 8])
        scv = GT("scv", [128, 8])
        adabg = GT("adabg", [128, 96])
        cstE = GT("cstE", [8, 1024], BF16)
        idA = GT("idA", [128, 512], BF16)
        idB = GT("idB", [128, 512], BF16)
        H = GT("H", [128, 8, S], BF16)
        OT = GT("OT", [128, 8, S], BF16)
        Hk = [f"H{c}" for c in range(8)]
        ps = [glob.enter_context(nc.psum_tensor(f"ps{i}", [128, 512], F32)) for i in range(8)]

        class _XH:
            t = None
            st = None
        XX = _XH()

        def push_X():
            XX.st = ExitStack()
            XX.t = alloc(XX.st, "X", [128, 8, S], F32)

        def pop_X():
            XX.st.close()
            XX.t = None
        push_X()
        Xk = [[f"X{c}_{g}" for g in range(NG)] for c in range(8)]
        ident = cstb[:, C_ID:C_ID + 128]
        Rm = cstb[:, C_RM:C_RM + 128]
        triA = cstb[:, C_TRIA:C_TRIA + 128]
        triB = cstb[:, C_TRIB:C_TRIB + 128]
        allneg = cstb[:, C_ALLNEG:C_ALLNEG + 128]
        trif = cstf[:, C_TRIF:C_TRIF + 128]
        onesf = cstf[:, C_ONES:C_ONES + 128]

        rot = {"g": 0, "o": 0}

        def gbank():
            rot["g"] = (rot["g"] + 1) % 4
            return rot["g"]

        def obank():
            rot["o"] = (rot["o"] + 1) % 4
            return 4 + rot["o"]

        def mm(out, lhsT, rhs, start, stop, r, w):
            P.op("pe", lambda e: e.matmul(out, lhsT, rhs, start=start, stop=stop), r, w)

        def act(out, in_, func, r, w, scale=1.0, bias=0.0):
            P.op("act", lambda e: e.activation(out=out, in_=in_, func=func, bias=bias, scale=scale), r, w)

        def tt(eng, out, a, b, op, r, w):
            P.op(eng, lambda e: e.tensor_tensor(out, a, b, op=op), r, w)

        def ts(out, a, s1, s2, op0, op1, r, w, accum=None, eng="dve"):
            if op1 is None:
                P.op(eng, lambda e: e.tensor_scalar(out, a, s1, None, op0=op0), r, w)
            elif accum is None:
                P.op(eng, lambda e: e.tensor_scalar(out, a, s1, s2, op0=op0, op1=op1), r, w)
            else:
                P.op(eng, lambda e: e.tensor_scalar(out, a, s1, s2, op0=op0, op1=op1, accum_out=accum), r, w)

        def stt(out, in0, scalar, in1, op0, op1, r, w):
            P.op("dve", lambda e: e.scalar_tensor_tensor(out=out, in0=in0, scalar=scalar, in1=in1, op0=op0, op1=op1), r, w)

        def cp(eng, out, in_, r, w):
            P.op(eng, lambda e: e.tensor_copy(out, in_), r, w)

        def end_phase():
            P.barrier()
            P.flush()

        def finish_with(src_key="X"):
            for c in range(8):
                P.dma("sp", out_d[c * 128:(c + 1) * 128, :], XX.t[:, c, :], reads=Xk[c], writes=["out"], slot="out")
            for name, (ap, dap) in dbg_outs.items():
                P.dma("pool", dap, ap, reads=[], writes=["dbgout"], slot="dbgout")
            P.barrier()
            P.flush()
            pop_X()

        def dbg_out(name, ap, shape):
            if dbg:
                dap = nc.dram_tensor("dbg_" + name, shape, F32, kind="ExternalOutput").ap()
                dbg_outs[name] = (ap, dap)

        def ada_mod(T, ls_list, adab):
            wblk = [T(f"adawblk{i}", [128, 8, 512]) for i in range(2)]
            bi = 0
            for ls in ls_list:
                pb = gbank()
                for cb in range(6):
                    wb = wblk[bi % 2]
                    wk = f"adawblk{bi % 2}"
                    bi += 1
                    src = adaw_d[ls, :, cb * 512:(cb + 1) * 512].rearrange("(kc p) n -> p kc n", p=128)
                    P.dma("sp", wb[:], src, writes=[wk])
                    for o4 in range(4):
                        col = cb * 4 + o4
                        for kc in range(8):
                            mm(ps[pb][:, col:col + 1], wb[:, kc, o4 * 128:(o4 + 1) * 128], scv[:, kc:kc + 1],
                               kc == 0, kc == 7, [wk, "sc"], [f"ps{pb}"])
                b = ls * 24
                tt("dve", modv[:, b:b + 24], ps[pb][:, 0:24], adab[:, b:b + 24], ALU.add, [f"ps{pb}", "adab"], ["modv"])
                cp("dve", mod1[:, b:b + 8], modv[:, b:b + 8], ["modv"], ["mod1"])
                ts(mod1[:, b + 8:b + 16], modv[:, b + 8:b + 16], 1.0, None, ALU.add, None, ["modv"], ["mod1"])
                ts(mod1[:, b + 16:b + 24], modv[:, b + 16:b + 24], 1.0, float(1.0 / ALPHA), ALU.add, ALU.mult, ["modv"], ["mod1"])

        def weight_precast_jobs():
            jobs = []
            jobs.append((wos_d[0].rearrange("p (kc n) -> p kc n", n=D), wout_d[0].rearrange("(kc p) n -> p kc n", p=128), "wcastO0"))
            cwv = cwin_d.rearrange("(kc p) n -> p kc n", p=128)
            for c in range(8):
                for part in range(3):
                    jobs.append((cws_d[c].rearrange("p (kc n) -> p kc n", n=384)[:, :, part * 128:(part + 1) * 128],
                                 cwv[:, :, part * 1024 + c * 128:part * 1024 + (c + 1) * 128], "wcastC"))
            jobs.append((wos_d[1].rearrange("p (kc n) -> p kc n", n=D), wout_d[1].rearrange("(kc p) n -> p kc n", p=128), "wcastO1"))
            for layer in range(2):
                for blk in range(8):
                    jobs.append((w1s_d[layer, blk].rearrange("p (kc n) -> p kc n", n=512),
                                 w1_d[layer][:, blk * 512:(blk + 1) * 512].rearrange("(kc p) n -> p kc n", p=128), f"wcast{layer}"))
                for blk in range(8):
                    jobs.append((w2s_d[layer, blk].rearrange("p (kc n) -> p kc n", n=128),
                                 w2_d[layer][:, blk * 128:(blk + 1) * 128].rearrange("(kc p) n -> p kc n", p=128), f"wcast{layer}"))
            return jobs

        precast_jobs = weight_precast_jobs()

        def issue_precast(n):
            for _ in range(n):
                if precast_jobs:
                    o_, i_, k_ = precast_jobs.pop(0)
                    P.dma("pool", o_, i_, writes=[k_])

        with ExitStack() as ph:
            T = lambda n, s, d=F32: alloc(ph, n, s, d)
            for c in range(8):
                P.dma("sp", XX.t[:, c, :], xT_d[c * 128:(c + 1) * 128, :], writes=Xk[c], slot=f"Xld{c}")
            P.dma("sp", cstf[:], cst_d[:, 0:C_EJ], writes=["cstf"])
            P.dma("pool", cstb[:], cst_d[:, 0:640], writes=["cstb"])
            for i in range(4):
                P.dma("pool", id4[:, i * 128:(i + 1) * 128], cst_d[:, C_ID:C_ID + 128], writes=["id4"])
            P.dma("pool", cstE[:], cst_d[0:8, C_EJ:C_EJ + 1024], writes=["cstE"])
            P.op("pool", lambda e: e.memset(idA[:], 0.0), [], ["idA"])
            P.op("pool", lambda e: e.memset(idB[:], 0.0), [], ["idB"])
            for i in (0, 2):
                P.dma("pool", idA[:, i * 128:(i + 1) * 128], cst_d[:, C_ID:C_ID + 128], reads=[], writes=["idA"])
                P.dma("pool", idB[:, (i + 1) * 128:(i + 2) * 128], cst_d[:, C_ID:C_ID + 128], reads=[], writes=["idB"])
            P.dma("sp", lng[:], lng_d[:, :], writes=["lng"])
            P.dma("sp", lnb[:], lnb_d[:, :], writes=["lnb"])
            P.dma("sp", qn[:], qn_d[:, :], writes=["qn"])
            posi = T("posi", [128, S], I32)
            P.dma("sp", posi[:], pos_d[0:1, :].to_broadcast([128, S]), writes=["posi"])
            posf = T("posf", [128, S])
            yv = T("yv", [128, S])
            yi = T("yi", [128, S], I32)
            yn = T("yn", [128, S])
            m1 = T("m1", [128, S])
            cp("dve", posf[:], posi[:], ["posi"], ["posf"])
            for which, dst in ((0, sinT), (1, cosT)):
                ts(yv[:], posf[:], cstf[:, C_INV:C_INV + 1], float(1.0 / (2 * np.pi)), ALU.mult, ALU.mult, ["posf", "cstf"], ["yv"])
                if which == 1:
                    ts(yv[:], yv[:], 0.25, None, ALU.add, None, ["yv"], ["yv"])
                cp("dve", yi[:], yv[:], ["yv"], ["yi"])
                cp("dve", yn[:], yi[:], ["yi"], ["yn"])
                tt("dve", yv[:], yv[:], yn[:], ALU.subtract, ["yv", "yn"], ["yv"])
                ts(m1[:], yv[:], 0.5, None, ALU.is_gt, None, ["yv"], ["m1"])
                tt("dve", yn[:], yv[:], m1[:], ALU.subtract, ["yv", "m1"], ["yn"])
                ts(m1[:], yn[:], -0.5, None, ALU.is_lt, None, ["yn"], ["m1"])
                tt("dve", yv[:], yn[:], m1[:], ALU.add, ["yn", "m1"], ["yv"])
                act(dst[:], yv[:], AF.Sin, ["yv"], ["cosT" if which else "sinT"], scale=float(2 * np.pi * (1 - 1e-6)))
            end_phase()
        with ExitStack() as ph:
            T = lambda n, s, d=F32: alloc(ph, n, s, d)
            adab = adabg
            P.dma("sp", adab[:], adab_d[:, :], writes=["adab"])
            cv = T("cv", [128, 8])
            P.dma("sp", cv[:], cvec_d[:, :], writes=["cv"])
            snk = T("snk", [128, 8])
            P.dma("sp", snk[:], sinks_d[0:1, :].to_broadcast([128, 8]), writes=["snk"])
            act(esink[:], snk[:], AF.Exp, ["snk"], ["esink"])
            act(scv[:], cv[:], AF.Silu, ["cv"], ["sc"])
            ada_mod(T, [0], adab)
            dbg_out("mod1", mod1[:], [128, 96])
            dbg_out("cosT", cosT[:], [128, S])
            dbg_out("sinT", sinT[:], [128, S])
            end_phase()
        if stop == "p0":
            finish_with()
            return nc

        def rope_block(bank, dst, tok0, n, tmpq, tmp1, tmp2, kq, k1, k2, dkeys):
            b2 = gbank()
            act(tmpq[:, 0:n], ps[bank][:, 0:n], AF.Copy, [f"ps{bank}"], [kq])
            mm(ps[b2][:, 0:n], Rm, tmpq[:, 0:n], True, True, [kq, "cstb"], [f"ps{b2}"])
            tt("dve", tmp1[:, 0:n], ps[bank][:, 0:n], cosT[:, tok0:tok0 + n], ALU.mult, [f"ps{bank}", "cosT"], [k1])
            tt("dve", tmp2[:, 0:n], ps[b2][:, 0:n], sinT[:, tok0:tok0 + n], ALU.mult, [f"ps{b2}", "sinT"], [k2])
            tt("dve", dst, tmp1[:, 0:n], tmp2[:, 0:n], ALU.add, [k1, k2], dkeys)

        def make_H(Hbuf, ls):
            b = ls * 24
            for c in range(8):
                for g in range(NG):
                    act(Hbuf[:, c, g * 512:(g + 1) * 512], XX.t[:, c, g * 512:(g + 1) * 512], AF.Identity,
                        [Xk[c][g], "mod1"], [f"H{c}"], scale=mod1[:, b + 8 + c:b + 9 + c], bias=mod1[:, b + c:b + c + 1])

        def layer_norm_group(ls, g, T):
            tk = slice(g * 512, (g + 1) * 512)
            X = XX.t
            b1, b2 = gbank(), gbank()
            for c in range(8):
                mm(ps[b1][:, :], onesf, X[:, c, tk], c == 0, c == 7, [Xk[c][g], "cstf"], [f"ps{b1}"])
            for c in range(8):
                sq = T["sq"][c % 2]
                act(sq[:], X[:, c, tk], AF.Square, [Xk[c][g]], [f"sq{c % 2}"])
                mm(ps[b2][:, :], onesf, sq[:], c == 0, c == 7, [f"sq{c % 2}", "cstf"], [f"ps{b2}"])
            mean, var, rstd, mrs = T["mean"], T["var"], T["rstd"], T["mrs"]
            ts(mean[:], ps[b1][:, :], 1.0 / D, None, ALU.mult, None, [f"ps{b1}"], ["ln_mean"])
            tt("dve", var[:], mean[:], mean[:], ALU.mult, ["ln_mean"], ["ln_var"])
            stt(var[:], ps[b2][:, :], 1.0 / D, var[:], ALU.mult, ALU.subtract, [f"ps{b2}", "ln_var"], ["ln_var"])
            act(var[:], var[:], AF.Sqrt, ["ln_var"], ["ln_var"], bias=float(LN_EPS_EFF))
            P.op("dve", lambda e: e.reciprocal(rstd[:], var[:]), ["ln_var"], ["ln_rstd"])
            tt("dve", mrs[:], mean[:], rstd[:], ALU.mult, ["ln_mean", "ln_rstd"], ["ln_mrs"])
            for c in range(8):
                t = T["lt"][c % 2]
                tt("dve", t[:], X[:, c, tk], rstd[:], ALU.mult, [Xk[c][g], "ln_rstd"], [f"lt{c % 2}"])
                tt("dve", t[:], t[:], mrs[:], ALU.subtract, [f"lt{c % 2}", "ln_mrs"], [f"lt{c % 2}"])
                act(X[:, c, tk], t[:], AF.Identity, [f"lt{c % 2}", "lng", "lnb"], [Xk[c][g]],
                    scale=lng[:, ls * 8 + c:ls * 8 + c + 1], bias=lnb[:, ls * 8 + c:ls * 8 + c + 1])

        def ln_tiles(ph):
            T = lambda n, s, d=F32: alloc(ph, n, s, d)
            return {"sq": [T("lnsq0", [128, 512]), T("lnsq1", [128, 512])],
                    "lt": [T("lnlt0", [128, 512]), T("lnlt1", [128, 512])],
                    "mean": T("lnmean", [128, 512]), "var": T("lnvar", [128, 512]),
                    "rstd": T("lnrstd", [128, 512]), "mrs": T("lnmrs", [128, 512])}

        def out_proj_residual_ln(ls, wdram, OT, ph):
            T = lambda n, s, d=F32: alloc(ph, n, s, d)
            wo = T("wo", [128, 8, D], BF16)
            wi_ = 0 if ls == 0 else 1
            P.dma("sp", wo[:], wos_d[wi_].rearrange("p (kc n) -> p kc n", n=D), reads=[f"wcastO{wi_}"], writes=["wo"])
            LT = ln_tiles(ph)
            gcol = ls * 24 + 16
            X = XX.t
            def proj_(g):
                tk = slice(g * 512, (g + 1) * 512)
                for dc in range(8):
                    b = gbank()
                    for kc in range(8):
                        mm(ps[b][:, :], wo[:, kc, dc * 128:(dc + 1) * 128], OT[:, kc, tk], kc == 0, kc == 7,
                           ["wo", f"OT{kc}"], [f"ps{b}"])
                    stt(X[:, dc, tk], ps[b][:, :], mod1[:, gcol + dc:gcol + dc + 1], X[:, dc, tk], ALU.mult, ALU.add,
                        [f"ps{b}", "mod1", Xk[dc][g]], [Xk[dc][g]])
            proj_(0)
            for g in range(NG):
                if g + 1 < NG:
                    proj_(g + 1)
                layer_norm_group(ls, g, LT)

        def spill_X():
            for c in range(8):
                P.dma("sp", xsp_d[c * 128:(c + 1) * 128, :], XX.t[:, c, :], reads=Xk[c], writes=["xspill"], slot="xspill")

        def reload_X(src=None):
            src = xsp_d if src is None else src
            for c in range(8):
                P.dma("sp", XX.t[:, c, :], src[c * 128:(c + 1) * 128, :], reads=["xspill"], writes=Xk[c], slot=f"Xld{c}")

        def run_pipeline(items, emit_S, emit_PV, depth=2, bg=None):
            st = {}
            nxt = 0
            n_items = len(items)
            bg = bg if bg is not None else []
            n_bg = len(bg)
            done_bg = 0
            for n in range(n_items):
                while nxt < n_items and nxt <= n + depth:
                    st[nxt] = emit_S(items[nxt])
                    nxt += 1
                emit_PV(items[n], st.pop(n))
                want = ((n + 1) * n_bg + n_items - 1) // n_items
                while done_bg < min(want, n_bg):
                    bg[done_bg]()
                    done_bg += 1
            while done_bg < n_bg:
                bg[done_bg]()
                done_bg += 1

        nrm = {"i": 0}

        def attn_normalize(ob, heads_dst, bufs, esb=None):
            j = nrm["i"] % len(bufs)
            nrm["i"] += 1
            osb, lsb, rl = bufs[j]
            ko, kl, kr = f"osb{j}", f"lsb{j}", f"rl{j}"
            act(osb[0:64, :], ps[ob][0:64, :], AF.Copy, [f"ps{ob}"], [ko])
            act(lsb[0:64, :], ps[ob][64:128, :], AF.Copy, [f"ps{ob}"], [kl])
            if esb is not None:
                tt("pool", lsb[0:64, :], lsb[0:64, :], esb[0:64, :], ALU.add, [kl, "esb"], [kl])
            P.op("dve", lambda e: e.reciprocal(rl[0:64, :], lsb[0:64, :]), [kl], [kr])
            for i, (dst, key) in enumerate(heads_dst):
                tt("pool", dst, osb[0:64, i * 128:(i + 1) * 128], rl[0:64, i * 128:(i + 1) * 128], ALU.mult,
                   [ko, kr], [key])

        if True:
            make_H(H, 0)
            end_phase()
            pop_X()
            with ExitStack() as ph:
                T = lambda n, s, d=F32: alloc(ph, n, s, d)
                wA = T("wA", [128, 8, 896], BF16)
                win_v = win_d.rearrange("(kc p) n -> p kc n", p=128)
                P.dma("pool", wA[:, :, 0:512], win_v[:, :, 0:512], writes=["wA"])
                for j in range(2):
                    for r in range(2):
                        P.dma("pool", wA[:, :, 512 + j * 128 + r * 64:512 + j * 128 + r * 64 + 64],
                              win_v[:, :, 512 + j * 64:512 + j * 64 + 64], writes=["wA"])
                P.dma("pool", wA[:, :, 768:896], win_v[:, :, 640:768], writes=["wA"])
                aqT = T("aqT", [128, 4, S], BF16)
                akT = T("akT", [128, 2, S], BF16)
                vaug = T("vaugA", [128, 2, NT, 128], BF16)
                tmpq = T("tmpq", [128, 512], BF16)
                tmp1 = T("tmp1", [128, 512])
                tmp2 = T("tmp2", [128, 512])
                PT = [T(f"PT{i}", [128, 512], BF16) for i in range(4)]
                nbufs = [(T("osb", [64, 512]), T("lsb", [64, 512]), T("rl", [64, 512])) for _ in range(2)]
                esb = [T(f"esb{u}", [64, 512]) for u in range(2)]
                for u in range(2):
                    for i in range(4):
                        ts(esb[u][:, i * 128:(i + 1) * 128], cstf[0:64, C_ONES:C_ONES + 128], esink[0:64, 2 * i + u:2 * i + u + 1], None,
                           ALU.mult, None, ["cstf", "esink"], ["esb"])
                P.op("pool", lambda e: e.memset(vaug[:], 1.0), [], ["vaugA"])
                for g in range(NG):
                    tk = slice(g * 512, (g + 1) * 512)
                    for oc in range(6):
                        b = gbank()
                        for kc in range(8):
                            mm(ps[b][:, :], wA[:, kc, oc * 128:(oc + 1) * 128], H[:, kc, tk], kc == 0, kc == 7,
                               ["wA", Hk[kc]], [f"ps{b}"])
                        if oc < 4:
                            rope_block(b, aqT[:, oc, tk], g * 512, 512, tmpq, tmp1, tmp2, "tmpq", "tmp1", "tmp2", ["aqT"])
                        else:
                            rope_block(b, akT[:, oc - 4, tk], g * 512, 512, tmpq, tmp1, tmp2, "tmpq", "tmp1", "tmp2", ["akT"])
                for t in range(NT):
                    b = gbank()
                    for kc in range(8):
                        mm(ps[b][:, 0:128], H[:, kc, t * 128:(t + 1) * 128], wA[:, kc, 768:896], kc == 0, kc == 7,
                           ["wA", Hk[kc]], [f"ps{b}"])
                    for j in range(2):
                        cp("dve", vaug[:, j, t, 0:64], ps[b][:, j * 64:(j + 1) * 64], [f"ps{b}"], ["vaugA"])
                ada_mod(T, [1, 2, 3], adabg)
                pstate = {"pti": 0, "ob": None}
                items = []
                for qt in range(NT):
                    for u in range(2):
                        kts = [k for k in (qt - 1, qt) if k >= 0]
                        for ki, kt in enumerate(kts):
                            items.append((qt, u, kt, ki == 0, ki == len(kts) - 1))

                def swa_S(it):
                    qt, u, kt, first, last = it
                    qs = slice(qt * 128, (qt + 1) * 128)
                    ks = slice(kt * 128, (kt + 1) * 128)
                    sb = gbank()
                    mm(ps[sb][:, :], triA if kt == qt else triB, id4[:, :], True, False, ["cstb", "id4"], [f"ps{sb}"])
                    hf = u * 64
                    for kvj in range(2):
                        mm(ps[sb][:, kvj * 256:(kvj + 1) * 256], akT[hf:hf + 64, kvj, ks], aqT[hf:hf + 64, 2 * kvj:2 * kvj + 2, qs],
                           False, kvj == 1, ["akT", "aqT"], [f"ps{sb}"])
                    i_ = pstate["pti"] % 4
                    pstate["pti"] += 1
                    act(PT[i_][:], ps[sb][:, :], AF.Exp, [f"ps{sb}"], [f"PT{i_}"], scale=0.125)
                    return i_

                def swa_PV(it, i_):
                    qt, u, kt, first, last = it
                    pstate["n"] = pstate.get("n", 0) + 1
                    if pstate["n"] % 4 == 0:
                        issue_precast(1)
                    qs = slice(qt * 128, (qt + 1) * 128)
                    if first:
                        pstate["ob"] = obank()
                    ob = pstate["ob"]
                    pt, pk = PT[i_], f"PT{i_}"
                    for kvj in range(2):
                        mm(ps[ob][:, kvj * 256:(kvj + 1) * 256], vaug[:, kvj, kt, :], pt[:, kvj * 256:(kvj + 1) * 256],
                           first and kvj == 0, last and kvj == 1, ["vaugA", pk], [f"ps{ob}"])
                    if last:
                        dsts = []
                        for i in range(4):
                            h = 2 * i + u
                            c, hf = h // 2, (h % 2) * 64
                            dsts.append((OT[hf:hf + 64, c, qs], f"OT{c}"))
                        attn_normalize(ob, dsts, nbufs, esb=esb[u])

                run_pipeline(items, swa_S, swa_PV)
                end_phase()
            if stop == "l0a":
                push_X()
                reload_X(xT_d)
                dbg_out("OT", OT[:, 0:4, :].rearrange("p a b -> p (a b)"), [128, 4 * S])
                finish_with()
                return nc
            with ExitStack() as ph:
                T = lambda n, s, d=F32: alloc(ph, n, s, d)
                win_v = win_d.rearrange("(kc p) n -> p kc n", p=128)
                wB = T("wB", [128, 8, 584], BF16)
                P.dma("pool", wB[:, :, 0:256], win_v[:, :, 768:1024], writes=["wB"])
                for r in range(2):
                    P.dma("pool", wB[:, :, 256 + r * 64:320 + r * 64], win_v[:, :, 1024:1088], writes=["wB"])
                    P.dma("pool", wB[:, :, 384 + r * 64:448 + r * 64], win_v[:, :, 1152:1216], writes=["wB"])
                P.dma("pool", wB[:, :, 512:576], win_v[:, :, 1088:1152], writes=["wB"])
                P.dma("pool", wB[:, :, 576:584], win_v[:, :, 1216:1224], writes=["wB"])
                wU = T("wU", [128, 2, 1024], BF16)
                P.dma("pool", wU[:, :, 0:512], wuq_d.rearrange("(kc p) n -> p kc n", p=128), writes=["wU"])
                P.dma("pool", wU[:, :, 512:1024], wuiq_d.rearrange("(kc p) n -> p kc n", p=128), writes=["wU"])
                cqn = T("cqn", [128, 2, S], BF16)
                cqbuf = T("cqbuf", [128, 4, 512])
                cqf = [cqbuf[:, i, :] for i in range(2)]
                cqs = [cqbuf[:, 2 + i, :] for i in range(2)]
                bqT = T("bqT", [128, 4, S], BF16)
                biqT = T("biqT", [128, 4, S], BF16)
                bkT = T("bkT", [128, S], BF16)
                bikT = T("bikT", [128, S], BF16)
                vaug = T("vaugB", [128, NT, 128], BF16)
                iw = T("iw", [128, NT, 8])
                absw = T("absw", [128, NT, 8])
                sgn = T("sgn", [128, NT, 8])
                tmpq = T("tmpq", [128, 512], BF16)
                tmp1 = T("tmp1", [128, 512])
                tmp2 = T("tmp2", [128, 512])
                PT = [T(f"PT{i}", [128, 512], BF16) for i in range(4)]
                nbufs = [(T("osb", [64, 512]), T("lsb", [64, 512]), T("rl", [64, 512])) for _ in range(1)]
                acc = T("acc", [128, S])
                mbs = [T(f"mb{i}", [128, S], BF16) for i in range(2)]
                junk = T("junk", [128, S], BF16)
                rt = [T(f"rt{i}", [128, 512]) for i in range(2)]
                dg = cqbuf[:, 0:2, :].rearrange("p a (b c) -> p (a b) c", c=128)
                sm = T("sm", [128, 64])
                P.op("pool", lambda e: e.memset(vaug[:], 1.0), [], ["vaugB"])
                rstd_b = T("rstdb", [128, 512])
                for g in range(NG):
                    tk = slice(g * 512, (g + 1) * 512)
                    b3 = gbank()
                    for oc in range(2):
                        b = gbank()
                        for kc in range(8):
                            mm(ps[b][:, :], wB[:, kc, oc * 128:(oc + 1) * 128], H[:, kc, tk], kc == 0, kc == 7,
                               ["wB", Hk[kc]], [f"ps{b}"])
                        act(cqf[oc], ps[b][:, :], AF.Copy, [f"ps{b}"], [f"cqf{oc}"])
                        act(cqs[oc], ps[b][:, :], AF.Square, [f"ps{b}"], [f"cqs{oc}"])
                        mm(ps[b3][:, :], onesf, cqs[oc], oc == 0, oc == 1, [f"cqs{oc}", "cstf"], [f"ps{b3}"])
                    act(rstd_b[:], ps[b3][:, :], AF.Sqrt, [f"ps{b3}"], ["rstdb"], scale=1.0 / 256, bias=1e-6)
                    P.op("dve", lambda e: e.reciprocal(rstd_b[:], rstd_b[:]), ["rstdb"], ["rstdb"])
                    for oc in range(2):
                        tt("dve", cqf[oc], cqf[oc], rstd_b[:], ALU.mult, [f"cqf{oc}", "rstdb"], [f"cqf{oc}"])
                        ts(cqn[:, oc, tk], cqf[oc], qn[:, oc:oc + 1], None, ALU.mult, None, [f"cqf{oc}", "qn"], ["cqn"])
                    for oc, (dst, dk) in enumerate(((bkT, "bkT"), (bikT, "bikT"))):
                        b = gbank()
                        for kc in range(8):
                            mm(ps[b][:, :], wB[:, kc, 256 + oc * 128:384 + oc * 128], H[:, kc, tk], kc == 0, kc == 7,
                               ["wB", Hk[kc]], [f"ps{b}"])
                        rope_block(b, dst[:, tk], g * 512, 512, tmpq, tmp1, tmp2, "tmpq", "tmp1", "tmp2", [dk])
                    for oc in range(8):
                        b = gbank()
                        for kc in range(2):
                            mm(ps[b][:, :], wU[:, kc, oc * 128:(oc + 1) * 128], cqn[:, kc, tk], kc == 0, kc == 1,
                               ["wU", "cqn"], [f"ps{b}"])
                        if oc < 4:
                            rope_block(b, bqT[:, oc, tk], g * 512, 512, tmpq, tmp1, tmp2, "tmpq", "tmp1", "tmp2", ["bqT"])
                        else:
                            rope_block(b, biqT[:, oc - 4, tk], g * 512, 512, tmpq, tmp1, tmp2, "tmpq", "tmp1", "tmp2", ["biqT"])
                for t in range(NT):
                    b = gbank()
                    for kc in range(8):
                        mm(ps[b][:, 0:72], H[:, kc, t * 128:(t + 1) * 128], wB[:, kc, 512:584], kc == 0, kc == 7,
                           ["wB", Hk[kc]], [f"ps{b}"])
                    cp("dve", vaug[:, t, 0:64], ps[b][:, 0:64], [f"ps{b}"], ["vaugB"])
                    ts(iw[:, t, :], ps[b][:, 64:72], float(8 ** -0.5 * 64 ** -0.5), None, ALU.mult, None, [f"ps{b}"], ["iw"])
                iwf = iw[:].rearrange("p a b -> p (a b)")
                ts(sgn[:].rearrange("p a b -> p (a b)"), iwf, 0.0, 2.0, ALU.is_ge, ALU.mult, ["iw"], ["sgn"])
                ts(sgn[:].rearrange("p a b -> p (a b)"), sgn[:].rearrange("p a b -> p (a b)"), -1.0, None, ALU.add, None, ["sgn"], ["sgn"])
                tt("dve", absw[:].rearrange("p a b -> p (a b)"), iwf, sgn[:].rearrange("p a b -> p (a b)"), ALU.mult, ["iw", "sgn"], ["absw"])
                pstate = {"pti": 0, "rti": 0, "ob": None}
                NBIS = 11

                def dsa_prep(qt):
                    mbq, mbk = mbs[qt % 2], f"mb{qt % 2}"
                    qs = slice(qt * 128, (qt + 1) * 128)
                    L = (qt + 1) * 128
                    nseg = (L + 511) // 512
                    for ih in range(8):
                        ts(dg[:, ih, :], cstf[:, C_ID:C_ID + 128], sgn[:, qt, ih:ih + 1], None, ALU.mult, None, ["cstf", "sgn"], ["dg", "cqf0", "cqf1"], eng="pool")
                    for sg in range(nseg):
                        n = min(512, L - sg * 512)
                        ab = obank()

                        def raw_(ih):
                            c, hf = ih // 2, (ih % 2) * 64
                            b = gbank()
                            mm(ps[b][:, 0:n], biqT[hf:hf + 64, c, qs], bikT[hf:hf + 64, sg * 512:sg * 512 + n], True, True,
                               ["biqT", "bikT"], [f"ps{b}"])
                            ri = pstate["rti"] % 2
                            pstate["rti"] += 1
                            act(rt[ri][:, 0:n], ps[b][:, 0:n], AF.Relu, [f"ps{b}", "absw"], [f"rt{ri}"], scale=absw[:, qt, ih:ih + 1])
                            return ri
                        pend = raw_(0)
                        for ih in range(8):
                            cur = pend
                            if ih + 1 < 8:
                                pend = raw_(ih + 1)
                            mm(ps[ab][:, 0:n], dg[:, ih, :], rt[cur][:, 0:n], ih == 0, ih == 7, ["dg", f"rt{cur}"], [f"ps{ab}"])
                        act(acc[:, sg * 512:sg * 512 + n], ps[ab][:, 0:n], AF.Copy, [f"ps{ab}"], ["acc"])
                    tt("dve", acc[:, L - 128:L], acc[:, L - 128:L], trif, ALU.add, ["acc", "cstf"], ["acc"])
                    thr = sm[:, 0:1]
                    if qt < 2:
                        P.op("dve", (lambda o_: lambda e: e.memset(o_, -1e29))(thr), [], ["sm"])
                    else:
                        mx8, mn, w0, wk, mid, cnt, tq = sm[:, 8:16], sm[:, 1:2], sm[:, 2:3], sm[:, 16:40], sm[:, 3:4], sm[:, 4:5], sm[:, 5:6]
                        P.op("dve", (lambda o_, i_: lambda e: e.max(out=o_, in_=i_))(mx8, acc[:, 0:L]), ["acc"], ["sm"])
                        P.op("dve", (lambda o_, i_: lambda e: e.tensor_reduce(out=o_, in_=i_, axis=AX.X, op=ALU.min))(mn, acc[:, 0:L - 128]), ["acc", "sm"], ["sm"])
                        tt("dve", w0, sm[:, 8:9], mn, ALU.subtract, ["sm"], ["sm"])
                        ts(wk, cstf[:, C_POW2:C_POW2 + 24], w0, None, ALU.mult, None, ["sm", "cstf"], ["sm"])
                        cp("dve", thr, mn, ["sm"], ["sm"])
                        for k in range(NBIS):
                            tt("dve", mid, thr, wk[:, k:k + 1], ALU.add, ["sm"], ["sm"])
                            ts(junk[:, 0:L], acc[:, 0:L], mid, 0.0, ALU.is_ge, ALU.add, ["acc", "sm"], ["junk", "sm"], accum=cnt)
                            ts(tq, cnt, 256.0, wk[:, k:k + 1], ALU.is_ge, ALU.mult, ["sm"], ["sm"])
                            tt("dve", thr, thr, tq, ALU.add, ["sm"], ["sm"])
                    ts(mbq[:, 0:L], acc[:, 0:L], thr, NEG, ALU.is_lt, ALU.mult, ["acc", "sm"], [mbk])

                def dsa_S(it):
                    qt, u, kt = it
                    mbq, mbk = mbs[qt % 2], f"mb{qt % 2}"
                    qs = slice(qt * 128, (qt + 1) * 128)
                    ks = slice(kt * 128, (kt + 1) * 128)
                    sb = gbank()
                    mm(ps[sb][:, :], mbq[:, ks], id4[:, :], True, False, [mbk, "id4"], [f"ps{sb}"])
                    hf = u * 64
                    mm(ps[sb][:, :], bkT[hf:hf + 64, ks], bqT[hf:hf + 64, 0:4, qs], False, True, ["bkT", "bqT"], [f"ps{sb}"])
                    i_ = pstate["pti"] % 4
                    pstate["pti"] += 1
                    act(PT[i_][:], ps[sb][:, :], AF.Exp, [f"ps{sb}"], [f"PT{i_}"], scale=0.125)
                    return i_

                def dsa_PV(it, i_):
                    qt, u, kt = it
                    qs = slice(qt * 128, (qt + 1) * 128)
                    if kt == 0:
                        pstate["ob"] = obank()
                    ob = pstate["ob"]
                    mm(ps[ob][:, :], vaug[:, kt, :], PT[i_][:, :], kt == 0, kt == qt, ["vaugB", f"PT{i_}"], [f"ps{ob}"])
                    if kt == qt:
                        dsts = []
                        for i in range(4):
                            h = 2 * i + u
                            c, hf = 4 + h // 2, (h % 2) * 64
                            dsts.append((OT[hf:hf + 64, c, qs], f"OT{c}"))
                        attn_normalize(ob, dsts, nbufs)

                dsa_prep(0)
                for qt in range(NT):
                    issue_precast(3)
                    if qt + 1 < NT:
                        dsa_prep(qt + 1)
                    run_pipeline([(qt, u, kt) for u in range(2) for kt in range(qt + 1)], dsa_S, dsa_PV)
                issue_precast(1000)
                end_phase()
            push_X()
            with ExitStack() as ph:
                reload_X(xT_d)
                out_proj_residual_ln(0, wout_d[0], OT, ph)
                end_phase()
        if stop == "l0mix":
            finish_with()
            return nc

        def mlp_sublayer(layer):
            ls = layer * 2 + 1
            with ExitStack() as ph:
                T = lambda n, s, d=F32: alloc(ph, n, s, d)
                X = XX.t
                AT = OT[:].rearrange("p a (b c) -> p (a b) c", c=512)
                hg = [H[:, 2 * i:2 * i + 2, :].rearrange("p a (b c) -> p (a b) c", c=512) for i in range(2)]
                NW1, NW2 = 2, 3
                w1b = [H[:, 4 + 2 * i:6 + 2 * i, :].rearrange("p a (b c) -> p (a b) c", c=512) for i in range(NW1)]
                w2b = [T(f"w2b{i}", [128, 32, 128], BF16) for i in range(NW2)]
                rl_ = [T(f"relu{i}", [128, 512]) for i in range(2)]
                LT = ln_tiles(ph)
                b0 = ls * 24
                wi = 0
                ri = 0
                st_ = {"wi": 0, "ri": 0}

                def hg_(g):
                    tk = slice(g * 512, (g + 1) * 512)
                    hgt, hk = hg[g % 2], f"hg{g % 2}"
                    for c in range(8):
                        act(hgt[:, c, :], X[:, c, tk], AF.Identity, [Xk[c][g], "mod1"], [hk],
                            scale=mod1[:, b0 + 8 + c:b0 + 9 + c], bias=mod1[:, b0 + c:b0 + c + 1])

                def A_(g):
                    hgt, hk = hg[g % 2], f"hg{g % 2}"
                    for ob in range(8):
                        w, wk = w1b[st_["wi"] % NW1], f"w1b{st_['wi'] % NW1}"
                        st_["wi"] += 1
                        P.dma("sp", w, w1s_d[layer, ob].rearrange("p (kc n) -> p kc n", n=512), reads=[f"wcast{layer}"], writes=[wk])
                        for o4 in range(4):
                            b = gbank()
                            for kc in range(8):
                                mm(ps[b][:, :], w[:, kc, o4 * 128:(o4 + 1) * 128], hgt[:, kc, :], kc == 0, kc == 7,
                                   [wk, hk], [f"ps{b}"])
                            r_, rk = rl_[st_["ri"] % 2], f"relu{st_['ri'] % 2}"
                            st_["ri"] += 1
                            act(r_[:], ps[b][:, :], AF.Relu, [f"ps{b}"], [rk])
                            tt("pool", AT[:, ob * 4 + o4, :], r_[:], r_[:], ALU.mult, [rk], [f"AT{ob * 4 + o4}"])

                def B_(g):
                    tk = slice(g * 512, (g + 1) * 512)
                    for dc in range(8):
                        w, wk = w2b[dc % NW2], f"w2b{dc % NW2}"
                        P.dma("sp", w[:], w2s_d[layer, dc].rearrange("p (kc n) -> p kc n", n=128), reads=[f"wcast{layer}"], writes=[wk])
                        b = gbank()
                        for kc in range(32):
                            mm(ps[b][:, :], w[:, kc, :], AT[:, kc, :], kc == 0, kc == 31, [wk, f"AT{kc}"], [f"ps{b}"])
                        stt(X[:, dc, tk], ps[b][:, :], mod1[:, b0 + 16 + dc:b0 + 17 + dc], X[:, dc, tk], ALU.mult, ALU.add,
                            [f"ps{b}", "mod1", Xk[dc][g]], [Xk[dc][g]])

                hg_(0)
                A_(0)
                B_(0)
                for g in range(1, NG):
                    hg_(g)
                    A_(g)
                    layer_norm_group(ls, g - 1, LT)
                    B_(g)
                layer_norm_group(ls, NG - 1, LT)
                end_phase()

        mlp_sublayer(0)
        if stop == "l0":
            finish_with()
            return nc

        if True:
            make_H(H, 2)
            spill_X()
            end_phase()
            pop_X()
            with ExitStack() as ph:
                T = lambda n, s, d=F32: alloc(ph, n, s, d)
                cw_v = cwin_d.rearrange("(kc p) n -> p kc n", p=128)
                wC = [T(f"wC{i}", [128, 8, 384], BF16) for i in range(2)]
                qT = [T(f"qT{i}", [128, S], BF16) for i in range(2)]
                kT = [T(f"kT{i}", [128, S], BF16) for i in range(2)]
                vaug = [T(f"vaugC{i}", [128, 2, NT, 128], BF16) for i in range(2)]
                kmf = T("kmf", [128, 8])
                kmb = [T(f"kmb{i}", [128, 8], BF16) for i in range(2)]
                gt = T("gt", [128, 4, 8])
                mx8 = T("mx8", [128, 4, 8])
                bias = [T(f"mbias{i}", [128, 4, 8], BF16) for i in range(2)]
                tmpq = T("tmpq", [128, 512], BF16)
                tmp1 = T("tmp1", [128, 512])
                tmp2 = T("tmp2", [128, 512])
                PT = [T(f"PT{i}", [128, 512], BF16) for i in range(4)]
                nbufs = [(T("osb", [64, 512]), T("lsb", [64, 512]), T("rl", [64, 512])) for _ in range(2)]
                qbd = [T(f"qbd{i}", [128, 8, 512], BF16) for i in range(2)]
                biasT = [T(f"biasT{i}", [8, 8, 512], BF16) for i in range(2)]
                cstM = T("cstM", [128, 8, 32])
                for own in range(8):
                    for s4 in range(4):
                        cp("pool", cstM[:, own, s4 * 8:(s4 + 1) * 8], cstf[:, C_M1 + own * 8:C_M1 + own * 8 + 8], ["cstf"], ["cstM"])
                for i in range(2):
                    P.op("pool", (lambda v: lambda e: e.memset(v[:], 1.0))(vaug[i]), [], [f"vaugC{i}"])
                    P.op("pool", (lambda v: lambda e: e.memset(v[:], 0.0))(qbd[i]), [], [f"qbd{i}"])
                pstate = {"pti": 0, "bii": 0, "ob": None}

                def moba_prep_jobs(c):
                    s_ = c % 2
                    w, wk = wC[s_], f"wC{s_}"
                    q_, qk = qT[s_], f"qT{s_}"
                    k_, kk = kT[s_], f"kT{s_}"
                    va, vk = vaug[s_], f"vaugC{s_}"
                    km, kmk = kmb[s_], f"kmb{s_}"
                    bT, bTk = biasT[s_], f"biasT{s_}"
                    qb, qbk = qbd[s_], f"qbd{s_}"
                    jobs = []

                    def j_dma():
                        P.dma("sp", w[:], cws_d[c].rearrange("p (kc n) -> p kc n", n=384), reads=["wcastC"], writes=[wk])
                    jobs.append(j_dma)

                    def j_proj(g, part):
                        def f():
                            tk = slice(g * 512, (g + 1) * 512)
                            dst, dk = ((q_, qk), (k_, kk))[part]
                            b = gbank()
                            for kc in range(8):
                                mm(ps[b][:, :], w[:, kc, part * 128:(part + 1) * 128], H[:, kc, tk], kc == 0, kc == 7,
                                   [wk, Hk[kc]], [f"ps{b}"])
                            rope_block(b, dst[:, tk], g * 512, 512, tmpq, tmp1, tmp2, "tmpq", "tmp1", "tmp2", [dk])
                        return f
                    for g in range(NG):
                        for part in range(2):
                            jobs.append(j_proj(g, part))

                    def j_v(t):
                        def f():
                            b = gbank()
                            for kc in range(8):
                                mm(ps[b][:, 0:128], H[:, kc, t * 128:(t + 1) * 128], w[:, kc, 256:384], kc == 0, kc == 7,
                                   [wk, Hk[kc]], [f"ps{b}"])
                            for j in range(2):
                                cp("dve", va[:, j, t, 0:64], ps[b][:, j * 64:(j + 1) * 64], [f"ps{b}"], [vk])
                        return f
                    for t in range(NT):
                        jobs.append(j_v(t))

                    def j_qbd(Q):
                        def f():
                            for hh in range(2):
                                cp("pool", qb[hh * 64:(hh + 1) * 64, Q, hh * 256:(hh + 1) * 256],
                                   q_[hh * 64:(hh + 1) * 64, Q * 256:(Q + 1) * 256], [qk], [qbk])
                        return f
                    for Q in range(8):
                        jobs.append(j_qbd(Q))

                    def j_km():
                        P.op("dve", lambda e: e.tensor_reduce(out=kmf[:], in_=k_[:].rearrange("p (a b) -> p a b", b=256),
                                                              axis=AX.X, op=ALU.add), [kk], ["kmf"])
                        ts(km[:], kmf[:], 1.0 / 256, None, ALU.mult, None, ["kmf"], [kmk])
                    jobs.append(j_km)

                    def j_bias(Q):
                        def f():
                            bt, bk_ = bias[pstate["bii"] % 2], f"mbias{pstate['bii'] % 2}"
                            pstate["bii"] += 1
                            gb = gbank()
                            for hh in range(2):
                                if hh == 1:
                                    P.pe_sync()
                                for qi in range(2):
                                    hf = hh * 64
                                    col = (qi * 2 + hh) * 8
                                    mm(ps[gb][:, col:col + 8], q_[hf:hf + 64, (2 * Q + qi) * 128:(2 * Q + qi + 1) * 128],
                                       km[hf:hf + 64, :], qi == 0 and hh == 0, qi == 1 and hh == 1, [qk, kmk], [f"ps{gb}"])
                            tt("dve", gt[:].rearrange("p a b -> p (a b)"), ps[gb][:, 0:32], cstM[:, Q, :],
                               ALU.add, [f"ps{gb}", "cstM"], ["gt"])
                            for s4 in range(4):
                                P.op("dve", (lambda s4: lambda e: e.max(out=mx8[:, s4, :], in_=gt[:, s4, :]))(s4), ["gt"], ["mx8"])
                            for s4 in range(4):
                                ts(bt[:, s4, :], gt[:, s4, :], mx8[:, s4, 2:3], NEG, ALU.is_lt, ALU.mult, ["gt", "mx8"], [bk_])
                            tb = gbank()
                            for s4 in range(4):
                                qi, hh = s4 // 2, s4 % 2
                                cb = (hh * 2 + qi) * 128
                                mm(ps[tb][0:8, cb:cb + 128], bt[:, s4, :], ident, s4 == 0, s4 == 3, [bk_, "cstb"], [f"ps{tb}"])
                            act(bT[:, Q, :], ps[tb][0:8, :], AF.Copy, [f"ps{tb}"], [bTk])
                        return f
                    for Q in range(1, 8):
                        jobs.append(j_bias(Q))
                    return jobs

                def moba_attn(c):
                    s_ = c % 2
                    k_, kk = kT[s_], f"kT{s_}"
                    va, vk = vaug[s_], f"vaugC{s_}"
                    qb, qbk = qbd[s_], f"qbd{s_}"
                    bT, bTk = biasT[s_], f"biasT{s_}"

                    def S_(it):
                        Q, kt = it
                        ks = slice(kt * 128, (kt + 1) * 128)
                        sb = gbank()
                        if kt < 2 * Q:
                            j = kt // 2
                            mm(ps[sb][:, :], cstE[0:8, j * 128:(j + 1) * 128], bT[0:8, Q, :], True, False, ["cstE", bTk], [f"ps{sb}"])
                        elif kt == 2 * Q:
                            mm(ps[sb][:, :], triA, idA[:, :], True, False, ["cstb", "idA"], [f"ps{sb}"])
                        else:
                            mm(ps[sb][:, :], allneg, idA[:, :], True, False, ["cstb", "idA"], [f"ps{sb}"])
                            mm(ps[sb][:, :], triA, idB[:, :], False, False, ["cstb", "idB"], [f"ps{sb}"])
                        mm(ps[sb][:, :], k_[:, ks], qb[:, Q, :], False, True, [kk, qbk], [f"ps{sb}"])
                        i_ = pstate["pti"] % 4
                        pstate["pti"] += 1
                        act(PT[i_][:], ps[sb][:, :], AF.Exp, [f"ps{sb}"], [f"PT{i_}"], scale=0.125)
                        return i_

                    def PV_(it, i_):
                        Q, kt = it
                        nkt = 2 * Q + 2
                        if kt == 0:
                            pstate["ob"] = obank()
                        ob = pstate["ob"]
                        pt, pk = PT[i_], f"PT{i_}"
                        for hh in range(2):
                            mm(ps[ob][:, hh * 256:(hh + 1) * 256], va[:, hh, kt, :], pt[:, hh * 256:(hh + 1) * 256],
                               kt == 0 and hh == 0, kt == nkt - 1 and hh == 1, [vk, pk], [f"ps{ob}"])
                        if kt == nkt - 1:
                            dsts = []
                            for hh in range(2):
                                for qi in range(2):
                                    dsts.append((OT[hh * 64:hh * 64 + 64, c, (2 * Q + qi) * 128:(2 * Q + qi + 1) * 128], f"OT{c}"))
                            attn_normalize(ob, dsts, nbufs)

                    its = [(Q, kt) for Q in range(8) for kt in range(2 * Q + 2)]
                    run_pipeline(its, S_, PV_, bg=(moba_prep_jobs(c + 1) if c + 1 < 8 else None))

                for j_ in moba_prep_jobs(0):
                    j_()
                for c in range(8):
                    moba_attn(c)
                end_phase()
            push_X()
            with ExitStack() as ph:
                reload_X()
                out_proj_residual_ln(2, wout_d[1], OT, ph)
                end_phase()
        if stop == "l1mix":
            finish_with()
            return nc
        mlp_sublayer(1)
        finish_with()
    return nc


_NC_CACHE = {}


def _prep_inputs(x, c, positions, ab_w_in, ab_q_norm, ab_w_uq, ab_w_uiq, ab_sinks, ab_w_out,
                 c_w_in, c_w_out, ada_w, ada_b, ln_g, ln_b, mlp_w1, mlp_w2):
    f = lambda a: np.ascontiguousarray(np.asarray(a, dtype=np.float32))
    shared = {
        "cst": _consts(),
        "ada_w": f(ada_w).reshape(4, D, 3 * D),
        "ada_b": f(np.asarray(ada_b).reshape(4, 3, 8, 128).transpose(3, 0, 1, 2).reshape(128, 96)),
        "ln_g": f(np.asarray(ln_g).reshape(4, 8, 128).transpose(2, 0, 1).reshape(128, 32)),
        "ln_b": f(np.asarray(ln_b).reshape(4, 8, 128).transpose(2, 0, 1).reshape(128, 32)),
        "qnorm": f(np.asarray(ab_q_norm).reshape(2, 128).T),
        "sinks": f(np.asarray(ab_sinks).reshape(1, 8)),
        "ab_w_in": f(ab_w_in)[0], "ab_w_uq": f(ab_w_uq)[0], "ab_w_uiq": f(ab_w_uiq)[0],
        "ab_w_out": f(ab_w_out)[0], "c_w_in": f(c_w_in)[0], "c_w_out": f(c_w_out)[0],
        "mlp_w1": f(mlp_w1), "mlp_w2": f(mlp_w2),
    }
    x = np.asarray(x, dtype=np.float32)
    c = np.asarray(c, dtype=np.float32)
    positions = np.asarray(positions, dtype=np.int32)
    maps = []
    for b in range(8):
        m = dict(shared)
        m["xT"] = np.ascontiguousarray(x[b].T)
        m["pos"] = np.ascontiguousarray(positions[b].reshape(1, S))
        m["cvec"] = np.ascontiguousarray(c[b].reshape(8, 128).T)
        maps.append(m)
    return maps


def kernel(**inputs):
    maps = _prep_inputs(**inputs)
    if "nc" not in _NC_CACHE:
        _NC_CACHE["nc"] = build()
    res = run_bass_kernel_spmd(_NC_CACHE["nc"], maps, core_ids=list(range(8)))
    out = np.stack([np.ascontiguousarray(r["outT"].T) for r in res.results], axis=0)
    return out.astype(np.float32)
```

```python
import numpy as np
from contextlib import ExitStack
import concourse.bass as bass
import concourse.mybir as mybir
from concourse.bass_utils import run_bass_kernel_spmd

F32 = mybir.dt.float32
BF16 = mybir.dt.bfloat16
I32 = mybir.dt.int32
AF = mybir.ActivationFunctionType
ALU = mybir.AluOpType
AX = mybir.AxisListType

S = 2048
D = 1024
NT = 16
NG = 4
NEG = -32768.0
ALPHA = 4.0 ** 0.25
LN_EPS_EFF = 1e-5 / (ALPHA * ALPHA)
ENGS = ["pe", "dve", "act", "pool", "sp"]

C_ID, C_RM, C_TRIA, C_TRIB, C_ALLNEG, C_TRIF, C_ONES = 0, 128, 256, 384, 512, 640, 768
C_POW2, C_INV, C_M1, C_EJ, C_END = 896, 920, 921, 985, 985 + 1024


def _consts():
    c = np.zeros((128, C_END), np.float32)
    c[:, C_ID:C_ID + 128] = np.eye(128)
    for m in range(128):
        if (m % 64) < 32:
            c[m + 32, C_RM + m] = -1.0
        else:
            c[m - 32, C_RM + m] = 1.0
    j = np.arange(128)[:, None]
    i = np.arange(128)[None, :]
    c[:, C_TRIA:C_TRIA + 128] = np.where(i > j, NEG, 0.0)
    c[:, C_TRIB:C_TRIB + 128] = np.where(i <= j, NEG, 0.0)
    c[:, C_ALLNEG:C_ALLNEG + 128] = NEG
    c[:, C_TRIF:C_TRIF + 128] = np.where(i > j, -1e30, 0.0)
    c[:, C_ONES:C_ONES + 128] = 1.0
    c[:, C_POW2:C_POW2 + 24] = 2.0 ** (-(np.arange(24) + 1.0))
    inv = (10000.0 ** (-np.arange(0, 64, 2, dtype=np.float32) / 64)).astype(np.float32)
    c[:, C_INV] = inv[np.arange(128) % 32]
    for own in range(8):
        c[:, C_M1 + own * 8:C_M1 + own * 8 + 8] = np.where(np.arange(8) >= own, -1e30, 0.0)[None, :]
    for j in range(8):
        c[j, C_EJ + j * 128:C_EJ + (j + 1) * 128] = 1.0
    return c


class Prog:
    def __init__(self, nc, stack):
        self.nc = nc
        self.stack = stack
        self.ops = {e: [] for e in ENGS}
        self.sem = {e: stack.enter_context(nc.semaphore("s_" + e)) for e in ENGS}
        self.cnt = {e: 0 for e in ENGS}
        self.known = {e: {} for e in ENGS}
        self.last_w = {}
        self.readers = {}
        self.slots = {}
        self.ninst = 0

    def slot_sem(self, name):
        if name not in self.slots:
            self.slots[name] = [self.stack.enter_context(self.nc.semaphore("d_" + name)), 0]
        return self.slots[name]

    def _wait(self, eng, s, v):
        if self.known[eng].get(s.name, 0) >= v:
            return
        self.known[eng][s.name] = v
        self.ops[eng].append(lambda e: e.wait_ge(s, v))

    def _emit(self, eng, fn, reads, writes, ev_sem):
        writes = list(writes) + [k for k in reads if k.startswith("ps") and k not in writes]
        reads = [k for k in reads if not k.startswith("ps")]
        deps = {}
        for k in list(reads) + list(writes):
            ev = self.last_w.get(k)
            if ev is not None and (ev[0].name not in deps or deps[ev[0].name][1] < ev[1]):
                deps[ev[0].name] = ev
        for k in writes:
            for ev in self.readers.get(k, {}).values():
                if ev[0].name not in deps or deps[ev[0].name][1] < ev[1]:
                    deps[ev[0].name] = ev
        for s, v in deps.values():
            if eng == "pe" and s is self.sem["pe"]:
                continue
            self._wait(eng, s, v)
        if ev_sem is None:
            self.cnt[eng] += 1
            sem, inc = self.sem[eng], 1
            ev = (sem, self.cnt[eng])
        else:
            ev_sem[1] += 16
            sem, inc = ev_sem[0], 16
            ev = (sem, ev_sem[1])
        self.ops[eng].append(lambda e: fn(e).then_inc(sem, inc))
        self.ninst += 1
        for k in writes:
            self.last_w[k] = ev
            self.readers[k] = {}
        for k in reads:
            if k not in writes:
                self.readers.setdefault(k, {})[sem.name] = ev
        return ev

    def pe_sync(self):
        if self.cnt["pe"] > 0:
            self._wait("pe", self.sem["pe"], self.cnt["pe"])

    def op(self, eng, fn, reads=(), writes=()):
        return self._emit(eng, fn, reads, writes, None)

    def dma(self, eng, out, in_, reads=(), writes=(), slot=None):
        if slot is None:
            slot = str(writes[0])
        return self._emit(eng, lambda e: e.dma_start(out=out, in_=in_), reads, writes, self.slot_sem(slot))

    def barrier(self):
        for s, v in self.slots.values():
            self._wait("sp", s, v)
        for e2 in ENGS:
            if e2 != "sp" and self.cnt[e2] > 0:
                self._wait("sp", self.sem[e2], self.cnt[e2])
        self.cnt["sp"] += 1
        sem, v = self.sem["sp"], self.cnt["sp"]
        self.ops["sp"].append(lambda e: e.nop().then_inc(sem, 1))
        for e2 in ENGS:
            if e2 != "sp":
                self._wait(e2, sem, v)

    def flush(self):
        nc = self.nc
        ops = self.ops
        self.ops = {e: [] for e in ENGS}
        with nc.Block() as block:
            @block.tensor
            def _(e):
                for o in ops["pe"]:
                    o(e)

            @block.vector
            def _(e):
                for o in ops["dve"]:
                    o(e)

            @block.scalar
            def _(e):
                for o in ops["act"]:
                    o(e)

            @block.gpsimd
            def _(e):
                for o in ops["pool"]:
                    o(e)

            @block.sync
            def _(e):
                for o in ops["sp"]:
                    o(e)


def build(stop=None, dbg=False):
    nc = bass.Bass("TRN2", target_bir_lowering=False)
    di = lambda n, s, d=F32: nc.dram_tensor(n, s, d, kind="ExternalInput").ap()
    xT_d = di("xT", [D, S])
    pos_d = di("pos", [1, S], I32)
    cvec_d = di("cvec", [128, 8])
    cst_d = di("cst", [128, C_END])
    adaw_d = di("ada_w", [4, D, 3 * D])
    adab_d = di("ada_b", [128, 96])
    lng_d = di("ln_g", [128, 32])
    lnb_d = di("ln_b", [128, 32])
    qn_d = di("qnorm", [128, 2])
    sinks_d = di("sinks", [1, 8])
    win_d = di("ab_w_in", [D, 1224])
    wuq_d = di("ab_w_uq", [256, 512])
    wuiq_d = di("ab_w_uiq", [256, 512])
    wout_d = [di("ab_w_out", [D, D]), di("c_w_out", [D, D])]
    cwin_d = di("c_w_in", [D, 3 * D])
    w1_d = di("mlp_w1", [2, D, 4 * D])
    w2_d = di("mlp_w2", [2, 4 * D, D])
    out_d = nc.dram_tensor("outT", [D, S], F32, kind="ExternalOutput").ap()
    xsp_d = nc.dram_tensor("xspill", [D, S], F32, kind="Internal").ap()
    w1s_d = nc.dram_tensor("w1s", [2, 8, 128, 4096], BF16, kind="Internal").ap()
    w2s_d = nc.dram_tensor("w2s", [2, 8, 128, 4096], BF16, kind="Internal").ap()
    cws_d = nc.dram_tensor("cws", [8, 128, 8 * 384], BF16, kind="Internal").ap()
    wos_d = nc.dram_tensor("wos", [2, 128, 8 * D], BF16, kind="Internal").ap()
    dbg_outs = {}

    uid = [0]

    def alloc(stack, n, shape, d=F32):
        uid[0] += 1
        return stack.enter_context(nc.sbuf_tensor(f"{n}_{uid[0]}", shape, d))

    with ExitStack() as glob:
        P = Prog(nc, glob)
        GT = lambda n, s, d=F32: glob.enter_context(nc.sbuf_tensor(n, s, d))
        cosT = GT("cosT", [128, S])
        sinT = GT("sinT", [128, S])
        cstf = GT("cstf", [128, C_EJ])
        cstb = GT("cstb", [128, 640], BF16)
        id4 = GT("id4", [128, 512], BF16)
        modv = GT("modv", [128, 96])
        mod1 = GT("mod1", [128, 96])
        lng = GT("lng", [128, 32])
        lnb = GT("lnb", [128, 32])
        qn = GT("qn", [128, 2])
        esink = GT("esink", [128, 8])
        scv = GT("scv", [128, 8])
        adabg = GT("adabg", [128, 96])
        cstE = GT("cstE", [8, 1024], BF16)
        idA = GT("idA", [128, 512], BF16)
        idB = GT("idB", [128, 512], BF16)
        H = GT("H", [128, 8, S], BF16)
        OT = GT("OT", [128, 8, S], BF16)
        Hk = [f"H{c}" for c in range(8)]
        ps = [glob.enter_context(nc.psum_tensor(f"ps{i}", [128, 512], F32)) for i in range(8)]

        class _XH:
            t = None
            st = None
        XX = _XH()

        def push_X():
            XX.st = ExitStack()
            XX.t = alloc(XX.st, "X", [128, 8, S], F32)

        def pop_X():
            XX.st.close()
            XX.t = None
        push_X()
        Xk = [[f"X{c}_{g}" for g in range(NG)] for c in range(8)]
        ident = cstb[:, C_ID:C_ID + 128]
        Rm = cstb[:, C_RM:C_RM + 128]
        triA = cstb[:, C_TRIA:C_TRIA + 128]
        triB = cstb[:, C_TRIB:C_TRIB + 128]
        allneg = cstb[:, C_ALLNEG:C_ALLNEG + 128]
        trif = cstf[:, C_TRIF:C_TRIF + 128]
        onesf = cstf[:, C_ONES:C_ONES + 128]

        rot = {"g": 0, "o": 0}

        def gbank():
            rot["g"] = (rot["g"] + 1) % 4
            return rot["g"]

        def obank():
            rot["o"] = (rot["o"] + 1) % 4
            return 4 + rot["o"]

        def mm(out, lhsT, rhs, start, stop, r, w):
            P.op("pe", lambda e: e.matmul(out, lhsT, rhs, start=start, stop=stop), r, w)

        def act(out, in_, func, r, w, scale=1.0, bias=0.0):
            P.op("act", lambda e: e.activation(out=out, in_=in_, func=func, bias=bias, scale=scale), r, w)

        def tt(eng, out, a, b, op, r, w):
            P.op(eng, lambda e: e.tensor_tensor(out, a, b, op=op), r, w)

        def ts(out, a, s1, s2, op0, op1, r, w, accum=None, eng="dve"):
            if op1 is None:
                P.op(eng, lambda e: e.tensor_scalar(out, a, s1, None, op0=op0), r, w)
            elif accum is None:
                P.op(eng, lambda e: e.tensor_scalar(out, a, s1, s2, op0=op0, op1=op1), r, w)
            else:
                P.op(eng, lambda e: e.tensor_scalar(out, a, s1, s2, op0=op0, op1=op1, accum_out=accum), r, w)

        def stt(out, in0, scalar, in1, op0, op1, r, w):
            P.op("dve", lambda e: e.scalar_tensor_tensor(out=out, in0=in0, scalar=scalar, in1=in1, op0=op0, op1=op1), r, w)

        def cp(eng, out, in_, r, w):
            P.op(eng, lambda e: e.tensor_copy(out, in_), r, w)

        def end_phase():
            P.barrier()
            P.flush()

        def finish_with(src_key="X"):
            for c in range(8):
                P.dma("sp", out_d[c * 128:(c + 1) * 128, :], XX.t[:, c, :], reads=Xk[c], writes=["out"], slot="out")
            for name, (ap, dap) in dbg_outs.items():
                P.dma("pool", dap, ap, reads=[], writes=["dbgout"], slot="dbgout")
            P.barrier()
            P.flush()
            pop_X()

        def dbg_out(name, ap, shape):
            if dbg:
                dap = nc.dram_tensor("dbg_" + name, shape, F32, kind="ExternalOutput").ap()
                dbg_outs[name] = (ap, dap)

        def ada_mod(T, ls_list, adab):
            wblk = [T(f"adawblk{i}", [128, 8, 512]) for i in range(2)]
            bi = 0
            for ls in ls_list:
                pb = gbank()
                for cb in range(6):
                    wb = wblk[bi % 2]
                    wk = f"adawblk{bi % 2}"
                    bi += 1
                    src = adaw_d[ls, :, cb * 512:(cb + 1) * 512].rearrange("(kc p) n -> p kc n", p=128)
                    P.dma("sp", wb[:], src, writes=[wk])
                    for o4 in range(4):
                        col = cb * 4 + o4
                        for kc in range(8):
                            mm(ps[pb][:, col:col + 1], wb[:, kc, o4 * 128:(o4 + 1) * 128], scv[:, kc:kc + 1],
                               kc == 0, kc == 7, [wk, "sc"], [f"ps{pb}"])
                b = ls * 24
                tt("dve", modv[:, b:b + 24], ps[pb][:, 0:24], adab[:, b:b + 24], ALU.add, [f"ps{pb}", "adab"], ["modv"])
                cp("dve", mod1[:, b:b + 8], modv[:, b:b + 8], ["modv"], ["mod1"])
                ts(mod1[:, b + 8:b + 16], modv[:, b + 8:b + 16], 1.0, None, ALU.add, None, ["modv"], ["mod1"])
                ts(mod1[:, b + 16:b + 24], modv[:, b + 16:b + 24], 1.0, float(1.0 / ALPHA), ALU.add, ALU.mult, ["modv"], ["mod1"])

        def weight_precast_jobs():
            jobs = []
            jobs.append((wos_d[0].rearrange("p (kc n) -> p kc n", n=D), wout_d[0].rearrange("(kc p) n -> p kc n", p=128), "wcastO0"))
            cwv = cwin_d.rearrange("(kc p) n -> p kc n", p=128)
            for c in range(8):
                for part in range(3):
                    jobs.append((cws_d[c].rearrange("p (kc n) -> p kc n", n=384)[:, :, part * 128:(part + 1) * 128],
                                 cwv[:, :, part * 1024 + c * 128:part * 1024 + (c + 1) * 128], "wcastC"))
            jobs.append((wos_d[1].rearrange("p (kc n) -> p kc n", n=D), wout_d[1].rearrange("(kc p) n -> p kc n", p=128), "wcastO1"))
            for layer in range(2):
                for blk in range(8):
                    jobs.append((w1s_d[layer, blk].rearrange("p (kc n) -> p kc n", n=512),
                                 w1_d[layer][:, blk * 512:(blk + 1) * 512].rearrange("(kc p) n -> p kc n", p=128), f"wcast{layer}"))
                for blk in range(8):
                    jobs.append((w2s_d[layer, blk].rearrange("p (kc n) -> p kc n", n=128),
                                 w2_d[layer][:, blk * 128:(blk + 1) * 128].rearrange("(kc p) n -> p kc n", p=128), f"wcast{layer}"))
            return jobs

        precast_jobs = weight_precast_jobs()

        def issue_precast(n):
            for _ in range(n):
                if precast_jobs:
                    o_, i_, k_ = precast_jobs.pop(0)
                    P.dma("pool", o_, i_, writes=[k_])

        with ExitStack() as ph:
            T = lambda n, s, d=F32: alloc(ph, n, s, d)
            for c in range(8):
                P.dma("sp", XX.t[:, c, :], xT_d[c * 128:(c + 1) * 128, :], writes=Xk[c], slot=f"Xld{c}")
            P.dma("sp", cstf[:], cst_d[:, 0:C_EJ], writes=["cstf"])
            P.dma("pool", cstb[:], cst_d[:, 0:640], writes=["cstb"])
            for i in range(4):
                P.dma("pool", id4[:, i * 128:(i + 1) * 128], cst_d[:, C_ID:C_ID + 128], writes=["id4"])
            P.dma("pool", cstE[:], cst_d[0:8, C_EJ:C_EJ + 1024], writes=["cstE"])
            P.op("pool", lambda e: e.memset(idA[:], 0.0), [], ["idA"])
            P.op("pool", lambda e: e.memset(idB[:], 0.0), [], ["idB"])
            for i in (0, 2):
                P.dma("pool", idA[:, i * 128:(i + 1) * 128], cst_d[:, C_ID:C_ID + 128], reads=[], writes=["idA"])
                P.dma("pool", idB[:, (i + 1) * 128:(i + 2) * 128], cst_d[:, C_ID:C_ID + 128], reads=[], writes=["idB"])
            P.dma("sp", lng[:], lng_d[:, :], writes=["lng"])
            P.dma("sp", lnb[:], lnb_d[:, :], writes=["lnb"])
            P.dma("sp", qn[:], qn_d[:, :], writes=["qn"])
            posi = T("posi", [128, S], I32)
            P.dma("sp", posi[:], pos_d[0:1, :].to_broadcast([128, S]), writes=["posi"])
            posf = T("posf", [128, S])
            yv = T("yv", [128, S])
            yi = T("yi", [128, S], I32)
            yn = T("yn", [128, S])
            m1 = T("m1", [128, S])
            cp("dve", posf[:], posi[:], ["posi"], ["posf"])
            for which, dst in ((0, sinT), (1, cosT)):
                ts(yv[:], posf[:], cstf[:, C_INV:C_INV + 1], float(1.0 / (2 * np.pi)), ALU.mult, ALU.mult, ["posf", "cstf"], ["yv"])
                if which == 1:
                    ts(yv[:], yv[:], 0.25, None, ALU.add, None, ["yv"], ["yv"])
                cp("dve", yi[:], yv[:], ["yv"], ["yi"])
                cp("dve", yn[:], yi[:], ["yi"], ["yn"])
                tt("dve", yv[:], yv[:], yn[:], ALU.subtract, ["yv", "yn"], ["yv"])
                ts(m1[:], yv[:], 0.5, None, ALU.is_gt, None, ["yv"], ["m1"])
                tt("dve", yn[:], yv[:], m1[:], ALU.subtract, ["yv", "m1"], ["yn"])
                ts(m1[:], yn[:], -0.5, None, ALU.is_lt, None, ["yn"], ["m1"])
                tt("dve", yv[:], yn[:], m1[:], ALU.add, ["yn", "m1"], ["yv"])
                act(dst[:], yv[:], AF.Sin, ["yv"], ["cosT" if which else "sinT"], scale=float(2 * np.pi * (1 - 1e-6)))
            end_phase()
        with ExitStack() as ph:
            T = lambda n, s, d=F32: alloc(ph, n, s, d)
            adab = adabg
            P.dma("sp", adab[:], adab_d[:, :], writes=["adab"])
            cv = T("cv", [128, 8])
            P.dma("sp", cv[:], cvec_d[:, :], writes=["cv"])
            snk = T("snk", [128, 8])
            P.dma("sp", snk[:], sinks_d[0:1, :].to_broadcast([128, 8]), writes=["snk"])
            act(esink[:], snk[:], AF.Exp, ["snk"], ["esink"])
            act(scv[:], cv[:], AF.Silu, ["cv"], ["sc"])
            ada_mod(T, [0], adab)
            dbg_out("mod1", mod1[:], [128, 96])
            dbg_out("cosT", cosT[:], [128, S])
            dbg_out("sinT", sinT[:], [128, S])
            end_phase()
        if stop == "p0":
            finish_with()
            return nc

        def rope_block(bank, dst, tok0, n, tmpq, tmp1, tmp2, kq, k1, k2, dkeys):
            b2 = gbank()
            act(tmpq[:, 0:n], ps[bank][:, 0:n], AF.Copy, [f"ps{bank}"], [kq])
            mm(ps[b2][:, 0:n], Rm, tmpq[:, 0:n], True, True, [kq, "cstb"], [f"ps{b2}"])
            tt("dve", tmp1[:, 0:n], ps[bank][:, 0:n], cosT[:, tok0:tok0 + n], ALU.mult, [f"ps{bank}", "cosT"], [k1])
            tt("dve", tmp2[:, 0:n], ps[b2][:, 0:n], sinT[:, tok0:tok0 + n], ALU.mult, [f"ps{b2}", "sinT"], [k2])
            tt("dve", dst, tmp1[:, 0:n], tmp2[:, 0:n], ALU.add, [k1, k2], dkeys)

        def make_H(Hbuf, ls):
            b = ls * 24
            for c in range(8):
                for g in range(NG):
                    act(Hbuf[:, c, g * 512:(g + 1) * 512], XX.t[:, c, g * 512:(g + 1) * 512], AF.Identity,
                        [Xk[c][g], "mod1"], [f"H{c}"], scale=mod1[:, b + 8 + c:b + 9 + c], bias=mod1[:, b + c:b + c + 1])

        def layer_norm_group(ls, g, T):
            tk = slice(g * 512, (g + 1) * 512)
            X = XX.t
            b1, b2 = gbank(), gbank()
            for c in range(8):
                mm(ps[b1][:, :], onesf, X[:, c, tk], c == 0, c == 7, [Xk[c][g], "cstf"], [f"ps{b1}"])
            for c in range(8):
                sq = T["sq"][c % 2]
                act(sq[:], X[:, c, tk], AF.Square, [Xk[c][g]], [f"sq{c % 2}"])
                mm(ps[b2][:, :], onesf, sq[:], c == 0, c == 7, [f"sq{c % 2}", "cstf"], [f"ps{b2}"])
            mean, var, rstd, mrs = T["mean"], T["var"], T["rstd"], T["mrs"]
            ts(mean[:], ps[b1][:, :], 1.0 / D, None, ALU.mult, None, [f"ps{b1}"], ["ln_mean"])
            tt("dve", var[:], mean[:], mean[:], ALU.mult, ["ln_mean"], ["ln_var"])
            stt(var[:], ps[b2][:, :], 1.0 / D, var[:], ALU.mult, ALU.subtract, [f"ps{b2}", "ln_var"], ["ln_var"])
            act(var[:], var[:], AF.Sqrt, ["ln_var"], ["ln_var"], bias=float(LN_EPS_EFF))
            P.op("dve", lambda e: e.reciprocal(rstd[:], var[:]), ["ln_var"], ["ln_rstd"])
            tt("dve", mrs[:], mean[:], rstd[:], ALU.mult, ["ln_mean", "ln_rstd"], ["ln_mrs"])
            for c in range(8):
                t = T["lt"][c % 2]
                tt("dve", t[:], X[:, c, tk], rstd[:], ALU.mult, [Xk[c][g], "ln_rstd"], [f"lt{c % 2}"])
                tt("dve", t[:], t[:], mrs[:], ALU.subtract, [f"lt{c % 2}", "ln_mrs"], [f"lt{c % 2}"])
                act(X[:, c, tk], t[:], AF.Identity, [f"lt{c % 2}", "lng", "lnb"], [Xk[c][g]],
                    scale=lng[:, ls * 8 + c:ls * 8 + c + 1], bias=lnb[:, ls * 8 + c:ls * 8 + c + 1])

        def ln_tiles(ph):
            T = lambda n, s, d=F32: alloc(ph, n, s, d)
            return {"sq": [T("lnsq0", [128, 512]), T("lnsq1", [128, 512])],
                    "lt": [T("lnlt0", [128, 512]), T("lnlt1", [128, 512])],
                    "mean": T("lnmean", [128, 512]), "var": T("lnvar", [128, 512]),
                    "rstd": T("lnrstd", [128, 512]), "mrs": T("lnmrs", [128, 512])}

        def out_proj_residual_ln(ls, wdram, OT, ph):
            T = lambda n, s, d=F32: alloc(ph, n, s, d)
            wo = T("wo", [128, 8, D], BF16)
            wi_ = 0 if ls == 0 else 1
            P.dma("sp", wo[:], wos_d[wi_].rearrange("p (kc n) -> p kc n", n=D), reads=[f"wcastO{wi_}"], writes=["wo"])
            LT = ln_tiles(ph)
            gcol = ls * 24 + 16
            X = XX.t
            def proj_(g):
                tk = slice(g * 512, (g + 1) * 512)
                for dc in range(8):
                    b = gbank()
                    for kc in range(8):
                        mm(ps[b][:, :], wo[:, kc, dc * 128:(dc + 1) * 128], OT[:, kc, tk], kc == 0, kc == 7,
                           ["wo", f"OT{kc}"], [f"ps{b}"])
                    stt(X[:, dc, tk], ps[b][:, :], mod1[:, gcol + dc:gcol + dc + 1], X[:, dc, tk], ALU.mult, ALU.add,
                        [f"ps{b}", "mod1", Xk[dc][g]], [Xk[dc][g]])
            proj_(0)
            for g in range(NG):
                if g + 1 < NG:
                    proj_(g + 1)
                layer_norm_group(ls, g, LT)

        def spill_X():
            for c in range(8):
                P.dma("sp", xsp_d[c * 128:(c + 1) * 128, :], XX.t[:, c, :], reads=Xk[c], writes=["xspill"], slot="xspill")

        def reload_X(src=None):
            src = xsp_d if src is None else src
            for c in range(8):
                P.dma("sp", XX.t[:, c, :], src[c * 128:(c + 1) * 128, :], reads=["xspill"], writes=Xk[c], slot=f"Xld{c}")

        def run_pipeline(items, emit_S, emit_PV, depth=2, bg=None):
            st = {}
            nxt = 0
            n_items = len(items)
            bg = bg if bg is not None else []
            n_bg = len(bg)
            done_bg = 0
            for n in range(n_items):
                while nxt < n_items and nxt <= n + depth:
                    st[nxt] = emit_S(items[nxt])
                    nxt += 1
                emit_PV(items[n], st.pop(n))
                want = ((n + 1) * n_bg + n_items - 1) // n_items
                while done_bg < min(want, n_bg):
                    bg[done_bg]()
                    done_bg += 1
            while done_bg < n_bg:
                bg[done_bg]()
                done_bg += 1

        nrm = {"i": 0}

        def attn_normalize(ob, heads_dst, bufs, esb=None):
            j = nrm["i"] % len(bufs)
            nrm["i"] += 1
            osb, lsb, rl = bufs[j]
            ko, kl, kr = f"osb{j}", f"lsb{j}", f"rl{j}"
            act(osb[0:64, :], ps[ob][0:64, :], AF.Copy, [f"ps{ob}"], [ko])
            act(lsb[0:64, :], ps[ob][64:128, :], AF.Copy, [f"ps{ob}"], [kl])
            if esb is not None:
                tt("pool", lsb[0:64, :], lsb[0:64, :], esb[0:64, :], ALU.add, [kl, "esb"], [kl])
            P.op("dve", lambda e: e.reciprocal(rl[0:64, :], lsb[0:64, :]), [kl], [kr])
            for i, (dst, key) in enumerate(heads_dst):
                tt("pool", dst, osb[0:64, i * 128:(i + 1) * 128], rl[0:64, i * 128:(i + 1) * 128], ALU.mult,
                   [ko, kr], [key])

        if True:
            make_H(H, 0)
            end_phase()
            pop_X()
            with ExitStack() as ph:
                T = lambda n, s, d=F32: alloc(ph, n, s, d)
                wA = T("wA", [128, 8, 896], BF16)
                win_v = win_d.rearrange("(kc p) n -> p kc n", p=128)
                P.dma("pool", wA[:, :, 0:512], win_v[:, :, 0:512], writes=["wA"])
                for j in range(2):
                    for r in range(2):
                        P.dma("pool", wA[:, :, 512 + j * 128 + r * 64:512 + j * 128 + r * 64 + 64],
                              win_v[:, :, 512 + j * 64:512 + j * 64 + 64], writes=["wA"])
                P.dma("pool", wA[:, :, 768:896], win_v[:, :, 640:768], writes=["wA"])
                aqT = T("aqT", [128, 4, S], BF16)
                akT = T("akT", [128, 2, S], BF16)
                vaug = T("vaugA", [128, 2, NT, 128], BF16)
                tmpq = T("tmpq", [128, 512], BF16)
                tmp1 = T("tmp1", [128, 512])
                tmp2 = T("tmp2", [128, 512])
                PT = [T(f"PT{i}", [128, 512], BF16) for i in range(4)]
                nbufs = [(T("osb", [64, 512]), T("lsb", [64, 512]), T("rl", [64, 512])) for _ in range(2)]
                esb = [T(f"esb{u}", [64, 512]) for u in range(2)]
                for u in range(2):
                    for i in range(4):
                        ts(esb[u][:, i * 128:(i + 1) * 128], cstf[0:64, C_ONES:C_ONES + 128], esink[0:64, 2 * i + u:2 * i + u + 1], None,
                           ALU.mult, None, ["cstf", "esink"], ["esb"])
                P.op("pool", lambda e: e.memset(vaug[:], 1.0), [], ["vaugA"])
                for g in range(NG):
                    tk = slice(g * 512, (g + 1) * 512)
                    for oc in range(6):
                        b = gbank()
                        for kc in range(8):
                            mm(ps[b][:, :], wA[:, kc, oc * 128:(oc + 1) * 128], H[:, kc, tk], kc == 0, kc == 7,
                               ["wA", Hk[kc]], [f"ps{b}"])
                        if oc < 4:
                            rope_block(b, aqT[:, oc, tk], g * 512, 512, tmpq, tmp1, tmp2, "tmpq", "tmp1", "tmp2", ["aqT"])
                        else:
                            rope_block(b, akT[:, oc - 4, tk], g * 512, 512, tmpq, tmp1, tmp2, "tmpq", "tmp1", "tmp2", ["akT"])
                for t in range(NT):
                    b = gbank()
                    for kc in range(8):
                        mm(ps[b][:, 0:128], H[:, kc, t * 128:(t + 1) * 128], wA[:, kc, 768:896], kc == 0, kc == 7,
                           ["wA", Hk[kc]], [f"ps{b}"])
                    for j in range(2):
                        cp("dve", vaug[:, j, t, 0:64], ps[b][:, j * 64:(j + 1) * 64], [f"ps{b}"], ["vaugA"])
                ada_mod(T, [1, 2, 3], adabg)
                pstate = {"pti": 0, "ob": None}
                items = []
                for qt in range(NT):
                    for u in range(2):
                        kts = [k for k in (qt - 1, qt) if k >= 0]
                        for ki, kt in enumerate(kts):
                            items.append((qt, u, kt, ki == 0, ki == len(kts) - 1))

                def swa_S(it):
                    qt, u, kt, first, last = it
                    qs = slice(qt * 128, (qt + 1) * 128)
                    ks = slice(kt * 128, (kt + 1) * 128)
                    sb = gbank()
                    mm(ps[sb][:, :], triA if kt == qt else triB, id4[:, :], True, False, ["cstb", "id4"], [f"ps{sb}"])
                    hf = u * 64
                    for kvj in range(2):
                        mm(ps[sb][:, kvj * 256:(kvj + 1) * 256], akT[hf:hf + 64, kvj, ks], aqT[hf:hf + 64, 2 * kvj:2 * kvj + 2, qs],
                           False, kvj == 1, ["akT", "aqT"], [f"ps{sb}"])
                    i_ = pstate["pti"] % 4
                    pstate["pti"] += 1
                    act(PT[i_][:], ps[sb][:, :], AF.Exp, [f"ps{sb}"], [f"PT{i_}"], scale=0.125)
                    return i_

                def swa_PV(it, i_):
                    qt, u, kt, first, last = it
                    pstate["n"] = pstate.get("n", 0) + 1
                    if pstate["n"] % 4 == 0:
                        issue_precast(1)
                    qs = slice(qt * 128, (qt + 1) * 128)
                    if first:
                        pstate["ob"] = obank()
                    ob = pstate["ob"]
                    pt, pk = PT[i_], f"PT{i_}"
                    for kvj in range(2):
                        mm(ps[ob][:, kvj * 256:(kvj + 1) * 256], vaug[:, kvj, kt, :], pt[:, kvj * 256:(kvj + 1) * 256],
                           first and kvj == 0, last and kvj == 1, ["vaugA", pk], [f"ps{ob}"])
                    if last:
                        dsts = []
                        for i in range(4):
                            h = 2 * i + u
                            c, hf = h // 2, (h % 2) * 64
                            dsts.append((OT[hf:hf + 64, c, qs], f"OT{c}"))
                        attn_normalize(ob, dsts, nbufs, esb=esb[u])

                run_pipeline(items, swa_S, swa_PV)
                end_phase()
            if stop == "l0a":
                push_X()
                reload_X(xT_d)
                dbg_out("OT", OT[:, 0:4, :].rearrange("p a b -> p (a b)"), [128, 4 * S])
                finish_with()
                return nc
            with ExitStack() as ph:
                T = lambda n, s, d=F32: alloc(ph, n, s, d)
                win_v = win_d.rearrange("(kc p) n -> p kc n", p=128)
                wB = T("wB", [128, 8, 584], BF16)
                P.dma("pool", wB[:, :, 0:256], win_v[:, :, 768:1024], writes=["wB"])
                for r in range(2):
                    P.dma("pool", wB[:, :, 256 + r * 64:320 + r * 64], win_v[:, :, 1024:1088], writes=["wB"])
                    P.dma("pool", wB[:, :, 384 + r * 64:448 + r * 64], win_v[:, :, 1152:1216], writes=["wB"])
                P.dma("pool", wB[:, :, 512:576], win_v[:, :, 1088:1152], writes=["wB"])
                P.dma("pool", wB[:, :, 576:584], win_v[:, :, 1216:1224], writes=["wB"])
                wU = T("wU", [128, 2, 1024], BF16)
                P.dma("pool", wU[:, :, 0:512], wuq_d.rearrange("(kc p) n -> p kc n", p=128), writes=["wU"])
                P.dma("pool", wU[:, :, 512:1024], wuiq_d.rearrange("(kc p) n -> p kc n", p=128), writes=["wU"])
                cqn = T("cqn", [128, 2, S], BF16)
                cqf = [T(f"cqf{i}", [128, 512]) for i in range(2)]
                cqs = [T(f"cqs{i}", [128, 512]) for i in range(2)]
                bqT = T("bqT", [128, 4, S], BF16)
                biqT = T("biqT", [128, 4, S], BF16)
                bkT = T("bkT", [128, S], BF16)
                bikT = T("bikT", [128, S], BF16)
                vaug = T("vaugB", [128, NT, 128], BF16)
                iw = T("iw", [128, NT, 8])
                absw = T("absw", [128, NT, 8])
                sgn = T("sgn", [128, NT, 8])
                tmpq = T("tmpq", [128, 512], BF16)
                tmp1 = T("tmp1", [128, 512])
                tmp2 = T("tmp2", [128, 512])
                PT = [T(f"PT{i}", [128, 512], BF16) for i in range(4)]
                nbufs = [(T("osb", [64, 512]), T("lsb", [64, 512]), T("rl", [64, 512])) for _ in range(1)]
                acc = T("acc", [128, S])
                mbs = [T(f"mb{i}", [128, S], BF16) for i in range(2)]
                junk = T("junk", [128, S], BF16)
                rt = [T(f"rt{i}", [128, 512]) for i in range(2)]
                sm = T("sm", [128, 64])
                P.op("pool", lambda e: e.memset(vaug[:], 1.0), [], ["vaugB"])
                rstd_b = T("rstdb", [128, 512])
                for g in range(NG):
                    tk = slice(g * 512, (g + 1) * 512)
                    b3 = gbank()
                    for oc in range(2):
                        b = gbank()
                        for kc in range(8):
                            mm(ps[b][:, :], wB[:, kc, oc * 128:(oc + 1) * 128], H[:, kc, tk], kc == 0, kc == 7,
                               ["wB", Hk[kc]], [f"ps{b}"])
                        act(cqf[oc][:], ps[b][:, :], AF.Copy, [f"ps{b}"], [f"cqf{oc}"])
                        act(cqs[oc][:], ps[b][:, :], AF.Square, [f"ps{b}"], [f"cqs{oc}"])
                        mm(ps[b3][:, :], onesf, cqs[oc][:], oc == 0, oc == 1, [f"cqs{oc}", "cstf"], [f"ps{b3}"])
                    act(rstd_b[:], ps[b3][:, :], AF.Sqrt, [f"ps{b3}"], ["rstdb"], scale=1.0 / 256, bias=1e-6)
                    P.op("dve", lambda e: e.reciprocal(rstd_b[:], rstd_b[:]), ["rstdb"], ["rstdb"])
                    for oc in range(2):
                        tt("dve", cqf[oc][:], cqf[oc][:], rstd_b[:], ALU.mult, [f"cqf{oc}", "rstdb"], [f"cqf{oc}"])
                        ts(cqn[:, oc, tk], cqf[oc][:], qn[:, oc:oc + 1], None, ALU.mult, None, [f"cqf{oc}", "qn"], ["cqn"])
                    for oc, (dst, dk) in enumerate(((bkT, "bkT"), (bikT, "bikT"))):
                        b = gbank()
                        for kc in range(8):
                            mm(ps[b][:, :], wB[:, kc, 256 + oc * 128:384 + oc * 128], H[:, kc, tk], kc == 0, kc == 7,
                               ["wB", Hk[kc]], [f"ps{b}"])
                        rope_block(b, dst[:, tk], g * 512, 512, tmpq, tmp1, tmp2, "tmpq", "tmp1", "tmp2", [dk])
                    for oc in range(8):
                        b = gbank()
                        for kc in range(2):
                            mm(ps[b][:, :], wU[:, kc, oc * 128:(oc + 1) * 128], cqn[:, kc, tk], kc == 0, kc == 1,
                               ["wU", "cqn"], [f"ps{b}"])
                        if oc < 4:
                            rope_block(b, bqT[:, oc, tk], g * 512, 512, tmpq, tmp1, tmp2, "tmpq", "tmp1", "tmp2", ["bqT"])
                        else:
                            rope_block(b, biqT[:, oc - 4, tk], g * 512, 512, tmpq, tmp1, tmp2, "tmpq", "tmp1", "tmp2", ["biqT"])
                for t in range(NT):
                    b = gbank()
                    for kc in range(8):
                        mm(ps[b][:, 0:72], H[:, kc, t * 128:(t + 1) * 128], wB[:, kc, 512:584], kc == 0, kc == 7,
                           ["wB", Hk[kc]], [f"ps{b}"])
                    cp("dve", vaug[:, t, 0:64], ps[b][:, 0:64], [f"ps{b}"], ["vaugB"])
                    ts(iw[:, t, :], ps[b][:, 64:72], float(8 ** -0.5 * 64 ** -0.5), None, ALU.mult, None, [f"ps{b}"], ["iw"])
                iwf = iw[:].rearrange("p a b -> p (a b)")
                ts(sgn[:].rearrange("p a b -> p (a b)"), iwf, 0.0, 2.0, ALU.is_ge, ALU.mult, ["iw"], ["sgn"])
                ts(sgn[:].rearrange("p a b -> p (a b)"), sgn[:].rearrange("p a b -> p (a b)"), -1.0, None, ALU.add, None, ["sgn"], ["sgn"])
                tt("dve", absw[:].rearrange("p a b -> p (a b)"), iwf, sgn[:].rearrange("p a b -> p (a b)"), ALU.mult, ["iw", "sgn"], ["absw"])
                pstate = {"pti": 0, "rti": 0, "ob": None}
                NBIS = 11

                def dsa_prep(qt):
                    mbq, mbk = mbs[qt % 2], f"mb{qt % 2}"
                    qs = slice(qt * 128, (qt + 1) * 128)
                    L = (qt + 1) * 128
                    nseg = (L + 511) // 512
                    for ih in range(8):
                        c, hf = ih // 2, (ih % 2) * 64
                        for sg in range(nseg):
                            n = min(512, L - sg * 512)
                            b = gbank()
                            mm(ps[b][:, 0:n], biqT[hf:hf + 64, c, qs], bikT[hf:hf + 64, sg * 512:sg * 512 + n], True, True,
                               ["biqT", "bikT"], [f"ps{b}"])
                            ri = pstate["rti"] % 2
                            pstate["rti"] += 1
                            r_, rk = rt[ri], f"rt{ri}"
                            act(r_[:, 0:n], ps[b][:, 0:n], AF.Relu, [f"ps{b}", "absw"], [rk], scale=absw[:, qt, ih:ih + 1])
                            seg = acc[:, sg * 512:sg * 512 + n]
                            if ih == 0:
                                ts(seg, r_[:, 0:n], sgn[:, qt, ih:ih + 1], None, ALU.mult, None, [rk, "sgn"], ["acc"])
                            else:
                                stt(seg, r_[:, 0:n], sgn[:, qt, ih:ih + 1], seg, ALU.mult, ALU.add, [rk, "sgn", "acc"], ["acc"])
                    tt("dve", acc[:, L - 128:L], acc[:, L - 128:L], trif, ALU.add, ["acc", "cstf"], ["acc"])
                    thr = sm[:, 0:1]
                    if qt < 2:
                        P.op("dve", (lambda o_: lambda e: e.memset(o_, -1e29))(thr), [], ["sm"])
                    else:
                        mx8, mn, w0, wk, mid, cnt, tq = sm[:, 8:16], sm[:, 1:2], sm[:, 2:3], sm[:, 16:40], sm[:, 3:4], sm[:, 4:5], sm[:, 5:6]
                        P.op("dve", (lambda o_, i_: lambda e: e.max(out=o_, in_=i_))(mx8, acc[:, 0:L]), ["acc"], ["sm"])
                        P.op("dve", (lambda o_, i_: lambda e: e.tensor_reduce(out=o_, in_=i_, axis=AX.X, op=ALU.min))(mn, acc[:, 0:L - 128]), ["acc", "sm"], ["sm"])
                        tt("dve", w0, sm[:, 8:9], mn, ALU.subtract, ["sm"], ["sm"])
                        ts(wk, cstf[:, C_POW2:C_POW2 + 24], w0, None, ALU.mult, None, ["sm", "cstf"], ["sm"])
                        cp("dve", thr, mn, ["sm"], ["sm"])
                        for k in range(NBIS):
                            tt("dve", mid, thr, wk[:, k:k + 1], ALU.add, ["sm"], ["sm"])
                            ts(junk[:, 0:L], acc[:, 0:L], mid, 0.0, ALU.is_ge, ALU.add, ["acc", "sm"], ["junk", "sm"], accum=cnt)
                            ts(tq, cnt, 256.0, wk[:, k:k + 1], ALU.is_ge, ALU.mult, ["sm"], ["sm"])
                            tt("dve", thr, thr, tq, ALU.add, ["sm"], ["sm"])
                    ts(mbq[:, 0:L], acc[:, 0:L], thr, NEG, ALU.is_lt, ALU.mult, ["acc", "sm"], [mbk])

                def dsa_S(it):
                    qt, u, kt = it
                    mbq, mbk = mbs[qt % 2], f"mb{qt % 2}"
                    qs = slice(qt * 128, (qt + 1) * 128)
                    ks = slice(kt * 128, (kt + 1) * 128)
                    sb = gbank()
                    mm(ps[sb][:, :], mbq[:, ks], id4[:, :], True, False, [mbk, "id4"], [f"ps{sb}"])
                    hf = u * 64
                    mm(ps[sb][:, :], bkT[hf:hf + 64, ks], bqT[hf:hf + 64, 0:4, qs], False, True, ["bkT", "bqT"], [f"ps{sb}"])
                    i_ = pstate["pti"] % 4
                    pstate["pti"] += 1
                    act(PT[i_][:], ps[sb][:, :], AF.Exp, [f"ps{sb}"], [f"PT{i_}"], scale=0.125)
                    return i_

                def dsa_PV(it, i_):
                    qt, u, kt = it
                    qs = slice(qt * 128, (qt + 1) * 128)
                    if kt == 0:
                        pstate["ob"] = obank()
                    ob = pstate["ob"]
                    mm(ps[ob][:, :], vaug[:, kt, :], PT[i_][:, :], kt == 0, kt == qt, ["vaugB", f"PT{i_}"], [f"ps{ob}"])
                    if kt == qt:
                        dsts = []
                        for i in range(4):
                            h = 2 * i + u
                            c, hf = 4 + h // 2, (h % 2) * 64
                            dsts.append((OT[hf:hf + 64, c, qs], f"OT{c}"))
                        attn_normalize(ob, dsts, nbufs)

                dsa_prep(0)
                for qt in range(NT):
                    issue_precast(3)
                    if qt + 1 < NT:
                        dsa_prep(qt + 1)
                    run_pipeline([(qt, u, kt) for u in range(2) for kt in range(qt + 1)], dsa_S, dsa_PV)
                issue_precast(1000)
                end_phase()
            push_X()
            with ExitStack() as ph:
                reload_X(xT_d)
                out_proj_residual_ln(0, wout_d[0], OT, ph)
                end_phase()
        if stop == "l0mix":
            finish_with()
            return nc

        def mlp_sublayer(layer):
            ls = layer * 2 + 1
            with ExitStack() as ph:
                T = lambda n, s, d=F32: alloc(ph, n, s, d)
                X = XX.t
                AT = OT[:].rearrange("p a (b c) -> p (a b) c", c=512)
                hg = [H[:, 2 * i:2 * i + 2, :].rearrange("p a (b c) -> p (a b) c", c=512) for i in range(2)]
                NW1, NW2 = 2, 3
                w1b = [H[:, 4 + 2 * i:6 + 2 * i, :].rearrange("p a (b c) -> p (a b) c", c=512) for i in range(NW1)]
                w2b = [T(f"w2b{i}", [128, 32, 128], BF16) for i in range(NW2)]
                rl_ = [T(f"relu{i}", [128, 512]) for i in range(2)]
                LT = ln_tiles(ph)
                b0 = ls * 24
                wi = 0
                ri = 0
                st_ = {"wi": 0, "ri": 0}

                def hg_(g):
                    tk = slice(g * 512, (g + 1) * 512)
                    hgt, hk = hg[g % 2], f"hg{g % 2}"
                    for c in range(8):
                        act(hgt[:, c, :], X[:, c, tk], AF.Identity, [Xk[c][g], "mod1"], [hk],
                            scale=mod1[:, b0 + 8 + c:b0 + 9 + c], bias=mod1[:, b0 + c:b0 + c + 1])

                def A_(g):
                    hgt, hk = hg[g % 2], f"hg{g % 2}"
                    for ob in range(8):
                        w, wk = w1b[st_["wi"] % NW1], f"w1b{st_['wi'] % NW1}"
                        st_["wi"] += 1
                        P.dma("sp", w, w1s_d[layer, ob].rearrange("p (kc n) -> p kc n", n=512), reads=[f"wcast{layer}"], writes=[wk])
                        for o4 in range(4):
                            b = gbank()
                            for kc in range(8):
                                mm(ps[b][:, :], w[:, kc, o4 * 128:(o4 + 1) * 128], hgt[:, kc, :], kc == 0, kc == 7,
                                   [wk, hk], [f"ps{b}"])
                            r_, rk = rl_[st_["ri"] % 2], f"relu{st_['ri'] % 2}"
                            st_["ri"] += 1
                            act(r_[:], ps[b][:, :], AF.Relu, [f"ps{b}"], [rk])
                            tt("pool", AT[:, ob * 4 + o4, :], r_[:], r_[:], ALU.mult, [rk], [f"AT{ob * 4 + o4}"])

                def B_(g):
                    tk = slice(g * 512, (g + 1) * 512)
                    for dc in range(8):
                        w, wk = w2b[dc % NW2], f"w2b{dc % NW2}"
                        P.dma("sp", w[:], w2s_d[layer, dc].rearrange("p (kc n) -> p kc n", n=128), reads=[f"wcast{layer}"], writes=[wk])
                        b = gbank()
                        for kc in range(32):
                            mm(ps[b][:, :], w[:, kc, :], AT[:, kc, :], kc == 0, kc == 31, [wk, f"AT{kc}"], [f"ps{b}"])
                        stt(X[:, dc, tk], ps[b][:, :], mod1[:, b0 + 16 + dc:b0 + 17 + dc], X[:, dc, tk], ALU.mult, ALU.add,
                            [f"ps{b}", "mod1", Xk[dc][g]], [Xk[dc][g]])

                hg_(0)
                A_(0)
                B_(0)
                for g in range(1, NG):
                    hg_(g)
                    A_(g)
                    layer_norm_group(ls, g - 1, LT)
                    B_(g)
                layer_norm_group(ls, NG - 1, LT)
                end_phase()

        mlp_sublayer(0)
        if stop == "l0":
            finish_with()
            return nc

        if True:
            make_H(H, 2)
            spill_X()
            end_phase()
            pop_X()
            with ExitStack() as ph:
                T = lambda n, s, d=F32: alloc(ph, n, s, d)
                cw_v = cwin_d.rearrange("(kc p) n -> p kc n", p=128)
                wC = [T(f"wC{i}", [128, 8, 384], BF16) for i in range(2)]
                qT = [T(f"qT{i}", [128, S], BF16) for i in range(2)]
                kT = [T(f"kT{i}", [128, S], BF16) for i in range(2)]
                vaug = [T(f"vaugC{i}", [128, 2, NT, 128], BF16) for i in range(2)]
                kmf = T("kmf", [128, 8])
                kmb = [T(f"kmb{i}", [128, 8], BF16) for i in range(2)]
                gt = T("gt", [128, 4, 8])
                mx8 = T("mx8", [128, 4, 8])
                bias = [T(f"mbias{i}", [128, 4, 8], BF16) for i in range(2)]
                tmpq = T("tmpq", [128, 512], BF16)
                tmp1 = T("tmp1", [128, 512])
                tmp2 = T("tmp2", [128, 512])
                PT = [T(f"PT{i}", [128, 512], BF16) for i in range(4)]
                nbufs = [(T("osb", [64, 512]), T("lsb", [64, 512]), T("rl", [64, 512])) for _ in range(2)]
                qbd = [T(f"qbd{i}", [128, 8, 512], BF16) for i in range(2)]
                biasT = [T(f"biasT{i}", [8, 8, 512], BF16) for i in range(2)]
                cstM = T("cstM", [128, 8, 32])
                for own in range(8):
                    for s4 in range(4):
                        cp("pool", cstM[:, own, s4 * 8:(s4 + 1) * 8], cstf[:, C_M1 + own * 8:C_M1 + own * 8 + 8], ["cstf"], ["cstM"])
                for i in range(2):
                    P.op("pool", (lambda v: lambda e: e.memset(v[:], 1.0))(vaug[i]), [], [f"vaugC{i}"])
                    P.op("pool", (lambda v: lambda e: e.memset(v[:], 0.0))(qbd[i]), [], [f"qbd{i}"])
                pstate = {"pti": 0, "bii": 0, "ob": None}

                def moba_prep_jobs(c):
                    s_ = c % 2
                    w, wk = wC[s_], f"wC{s_}"
                    q_, qk = qT[s_], f"qT{s_}"
                    k_, kk = kT[s_], f"kT{s_}"
                    va, vk = vaug[s_], f"vaugC{s_}"
                    km, kmk = kmb[s_], f"kmb{s_}"
                    bT, bTk = biasT[s_], f"biasT{s_}"
                    qb, qbk = qbd[s_], f"qbd{s_}"
                    jobs = []

                    def j_dma():
                        P.dma("sp", w[:], cws_d[c].rearrange("p (kc n) -> p kc n", n=384), reads=["wcastC"], writes=[wk])
                    jobs.append(j_dma)

                    def j_proj(g, part):
                        def f():
                            tk = slice(g * 512, (g + 1) * 512)
                            dst, dk = ((q_, qk), (k_, kk))[part]
                            b = gbank()
                            for kc in range(8):
                                mm(ps[b][:, :], w[:, kc, part * 128:(part + 1) * 128], H[:, kc, tk], kc == 0, kc == 7,
                                   [wk, Hk[kc]], [f"ps{b}"])
                            rope_block(b, dst[:, tk], g * 512, 512, tmpq, tmp1, tmp2, "tmpq", "tmp1", "tmp2", [dk])
                        return f
                    for g in range(NG):
                        for part in range(2):
                            jobs.append(j_proj(g, part))

                    def j_v(t):
                        def f():
                            b = gbank()
                            for kc in range(8):
                                mm(ps[b][:, 0:128], H[:, kc, t * 128:(t + 1) * 128], w[:, kc, 256:384], kc == 0, kc == 7,
                                   [wk, Hk[kc]], [f"ps{b}"])
                            for j in range(2):
                                cp("dve", va[:, j, t, 0:64], ps[b][:, j * 64:(j + 1) * 64], [f"ps{b}"], [vk])
                        return f
                    for t in range(NT):
                        jobs.append(j_v(t))

                    def j_qbd(Q):
                        def f():
                            for hh in range(2):
                                cp("pool", qb[hh * 64:(hh + 1) * 64, Q, hh * 256:(hh + 1) * 256],
                                   q_[hh * 64:(hh + 1) * 64, Q * 256:(Q + 1) * 256], [qk], [qbk])
                        return f
                    for Q in range(8):
                        jobs.append(j_qbd(Q))

                    def j_km():
                        P.op("dve", lambda e: e.tensor_reduce(out=kmf[:], in_=k_[:].rearrange("p (a b) -> p a b", b=256),
                                                              axis=AX.X, op=ALU.add), [kk], ["kmf"])
                        ts(km[:], kmf[:], 1.0 / 256, None, ALU.mult, None, ["kmf"], [kmk])
                    jobs.append(j_km)

                    def j_bias(Q):
                        def f():
                            bt, bk_ = bias[pstate["bii"] % 2], f"mbias{pstate['bii'] % 2}"
                            pstate["bii"] += 1
                            gb = gbank()
                            for hh in range(2):
                                if hh == 1:
                                    P.pe_sync()
                                for qi in range(2):
                                    hf = hh * 64
                                    col = (qi * 2 + hh) * 8
                                    mm(ps[gb][:, col:col + 8], q_[hf:hf + 64, (2 * Q + qi) * 128:(2 * Q + qi + 1) * 128],
                                       km[hf:hf + 64, :], qi == 0 and hh == 0, qi == 1 and hh == 1, [qk, kmk], [f"ps{gb}"])
                            tt("dve", gt[:].rearrange("p a b -> p (a b)"), ps[gb][:, 0:32], cstM[:, Q, :],
                               ALU.add, [f"ps{gb}", "cstM"], ["gt"])
                            for s4 in range(4):
                                P.op("dve", (lambda s4: lambda e: e.max(out=mx8[:, s4, :], in_=gt[:, s4, :]))(s4), ["gt"], ["mx8"])
                            for s4 in range(4):
                                ts(bt[:, s4, :], gt[:, s4, :], mx8[:, s4, 2:3], NEG, ALU.is_lt, ALU.mult, ["gt", "mx8"], [bk_])
                            tb = gbank()
                            for s4 in range(4):
                                qi, hh = s4 // 2, s4 % 2
                                cb = (hh * 2 + qi) * 128
                                mm(ps[tb][0:8, cb:cb + 128], bt[:, s4, :], ident, s4 == 0, s4 == 3, [bk_, "cstb"], [f"ps{tb}"])
                            act(bT[:, Q, :], ps[tb][0:8, :], AF.Copy, [f"ps{tb}"], [bTk])
                        return f
                    for Q in range(4, 8):
                        jobs.append(j_bias(Q))
                    return jobs

                def moba_attn(c):
                    s_ = c % 2
                    k_, kk = kT[s_], f"kT{s_}"
                    va, vk = vaug[s_], f"vaugC{s_}"
                    qb, qbk = qbd[s_], f"qbd{s_}"
                    bT, bTk = biasT[s_], f"biasT{s_}"

                    def S_(it):
                        Q, kt = it
                        ks = slice(kt * 128, (kt + 1) * 128)
                        sb = gbank()
                        nobias = kt < 2 * Q and Q <= 3
                        if nobias:
                            pass
                        elif kt < 2 * Q:
                            j = kt // 2
                            mm(ps[sb][:, :], cstE[0:8, j * 128:(j + 1) * 128], bT[0:8, Q, :], True, False, ["cstE", bTk], [f"ps{sb}"])
                        elif kt == 2 * Q:
                            mm(ps[sb][:, :], triA, idA[:, :], True, False, ["cstb", "idA"], [f"ps{sb}"])
                        else:
                            mm(ps[sb][:, :], allneg, idA[:, :], True, False, ["cstb", "idA"], [f"ps{sb}"])
                            mm(ps[sb][:, :], triA, idB[:, :], False, False, ["cstb", "idB"], [f"ps{sb}"])
                        mm(ps[sb][:, :], k_[:, ks], qb[:, Q, :], nobias, True, [kk, qbk], [f"ps{sb}"])
                        i_ = pstate["pti"] % 4
                        pstate["pti"] += 1
                        act(PT[i_][:], ps[sb][:, :], AF.Exp, [f"ps{sb}"], [f"PT{i_}"], scale=0.125)
                        return i_

                    def PV_(it, i_):
                        Q, kt = it
                        nkt = 2 * Q + 2
                        if kt == 0:
                            pstate["ob"] = obank()
                        ob = pstate["ob"]
                        pt, pk = PT[i_], f"PT{i_}"
                        for hh in range(2):
                            mm(ps[ob][:, hh * 256:(hh + 1) * 256], va[:, hh, kt, :], pt[:, hh * 256:(hh + 1) * 256],
                               kt == 0 and hh == 0, kt == nkt - 1 and hh == 1, [vk, pk], [f"ps{ob}"])
                        if kt == nkt - 1:
                            dsts = []
                            for hh in range(2):
                                for qi in range(2):
                                    dsts.append((OT[hh * 64:hh * 64 + 64, c, (2 * Q + qi) * 128:(2 * Q + qi + 1) * 128], f"OT{c}"))
                            attn_normalize(ob, dsts, nbufs)

                    its = [(Q, kt) for Q in range(8) for kt in range(2 * Q + 2)]
                    run_pipeline(its, S_, PV_, bg=(moba_prep_jobs(c + 1) if c + 1 < 8 else None))

                for j_ in moba_prep_jobs(0):
                    j_()
                for c in range(8):
                    moba_attn(c)
                end_phase()
            push_X()
            with ExitStack() as ph:
                reload_X()
                out_proj_residual_ln(2, wout_d[1], OT, ph)
                end_phase()
        if stop == "l1mix":
            finish_with()
            return nc
        mlp_sublayer(1)
        finish_with()
    return nc


_NC_CACHE = {}


def _prep_inputs(x, c, positions, ab_w_in, ab_q_norm, ab_w_uq, ab_w_uiq, ab_sinks, ab_w_out,
                 c_w_in, c_w_out, ada_w, ada_b, ln_g, ln_b, mlp_w1, mlp_w2):
    f = lambda a: np.ascontiguousarray(np.asarray(a, dtype=np.float32))
    shared = {
        "cst": _consts(),
        "ada_w": f(ada_w).reshape(4, D, 3 * D),
        "ada_b": f(np.asarray(ada_b).reshape(4, 3, 8, 128).transpose(3, 0, 1, 2).reshape(128, 96)),
        "ln_g": f(np.asarray(ln_g).reshape(4, 8, 128).transpose(2, 0, 1).reshape(128, 32)),
        "ln_b": f(np.asarray(ln_b).reshape(4, 8, 128).transpose(2, 0, 1).reshape(128, 32)),
        "qnorm": f(np.asarray(ab_q_norm).reshape(2, 128).T),
        "sinks": f(np.asarray(ab_sinks).reshape(1, 8)),
        "ab_w_in": f(ab_w_in)[0], "ab_w_uq": f(ab_w_uq)[0], "ab_w_uiq": f(ab_w_uiq)[0],
        "ab_w_out": f(ab_w_out)[0], "c_w_in": f(c_w_in)[0], "c_w_out": f(c_w_out)[0],
        "mlp_w1": f(mlp_w1), "mlp_w2": f(mlp_w2),
    }
    x = np.asarray(x, dtype=np.float32)
    c = np.asarray(c, dtype=np.float32)
    positions = np.asarray(positions, dtype=np.int32)
    maps = []
    for b in range(8):
        m = dict(shared)
        m["xT"] = np.ascontiguousarray(x[b].T)
        m["pos"] = np.ascontiguousarray(positions[b].reshape(1, S))
        m["cvec"] = np.ascontiguousarray(c[b].reshape(8, 128).T)
        maps.append(m)
    return maps


def kernel(**inputs):
    maps = _prep_inputs(**inputs)
    if "nc" not in _NC_CACHE:
        _NC_CACHE["nc"] = build()
    res = run_bass_kernel_spmd(_NC_CACHE["nc"], maps, core_ids=list(range(8)))
    out = np.stack([np.ascontiguousarray(r["outT"].T) for r in res.results], axis=0)
    return out.astype(np.float32)
```
